# Optimizing a Trainium2 kernel written in Bass

```python
import jax, jax.numpy as jnp
from jax import lax
import numpy as np

D_MODEL = 1024
BATCH = 32
SEQ = 2048
DEPTH = 1

CHUNK = 64
Q_BLOCK = 128
N_MEM = 256
EPS = 1e-6

HG_HEADS = 4
HG_DK = 128
HG_DV = 128
HG_WIDTH = HG_HEADS * HG_DK
FOX_HEADS = 8
FOX_DH = 64
FOX_WIDTH = FOX_HEADS * FOX_DH
MEM_HEADS = 4
MEM_DH = 128
MEM_WIDTH = MEM_HEADS * MEM_DH
N_BRANCH = 3
D_FF = 2816
CONV_W = 3

IN_SPLITS = (HG_WIDTH, HG_WIDTH, HG_WIDTH, HG_WIDTH,
             FOX_WIDTH, FOX_WIDTH, FOX_WIDTH, FOX_HEADS,
             MEM_WIDTH, N_BRANCH * D_MODEL)
IN_COLS = 4 * HG_WIDTH + 3 * FOX_WIDTH + FOX_HEADS + MEM_WIDTH + N_BRANCH * D_MODEL

kernel_name = "hybrid_hgrn2_fox_memory_convglu"


def _split_points():
    pts, acc = [], 0
    for s in IN_SPLITS[:-1]:
        acc += s
        pts.append(acc)
    return pts


def _rmsnorm(x, g):
    xf = x.astype(jnp.float32)
    return xf * lax.rsqrt(jnp.mean(xf * xf, axis=-1, keepdims=True) + EPS) * g.astype(jnp.float32)


def _hgrn2_mixer(q, f_logit, i, g_out, lb, norm_g):
    B, T, _ = q.shape
    n = T // CHUNK

    def heads(z):
        return z.reshape(B, n, CHUNK, HG_HEADS, -1).transpose(1, 0, 3, 2, 4)

    f = lb + (1.0 - lb) * jax.nn.sigmoid(f_logit.astype(jnp.float32))
    qh = heads(jax.nn.silu(q.astype(jnp.float32)))
    kh = heads(1.0 - f)
    ih = heads(i.astype(jnp.float32))
    G = jnp.cumsum(heads(jnp.log(f)), axis=3)
    causal = jnp.tril(jnp.ones((CHUNK, CHUNK), dtype=bool))[:, :, None]

    def step(S, inp):
        qc, kc, ic, Gc = inp
        diff = Gc[:, :, :, None, :] - Gc[:, :, None, :, :]
        decay = jnp.exp(jnp.where(causal, diff, -jnp.inf))
        A = jnp.einsum('bhtc,bhsc,bhtsc->bhts', qc, kc, decay)
        o = (jnp.einsum('bhts,bhsv->bhtv', A, ic)
             + jnp.einsum('bhtc,bhcv->bhtv', qc * jnp.exp(Gc), S))
        G_last = Gc[:, :, -1, :]
        S_new = (S * jnp.exp(G_last)[..., None]
                 + jnp.einsum('bhsc,bhsv->bhcv', kc * jnp.exp(G_last[:, :, None, :] - Gc), ic))
        return S_new, o

    S0 = jnp.zeros((B, HG_HEADS, HG_DK, HG_DV), jnp.float32)
    _, o = lax.scan(step, S0, (qh, kh, ih, G))
    o = o.transpose(1, 0, 3, 2, 4).reshape(B, T, HG_HEADS, HG_DV)
    o = _rmsnorm(o, norm_g).reshape(B, T, HG_WIDTH)
    return o * jax.nn.silu(g_out.astype(jnp.float32))


def _fox_mixer(q, k, v, f_logit, f_bias, q_g, k_g):
    B, T, _ = q.shape
    qh = _rmsnorm(q.reshape(B, T, FOX_HEADS, FOX_DH), q_g).transpose(0, 2, 1, 3)
    kh = _rmsnorm(k.reshape(B, T, FOX_HEADS, FOX_DH), k_g).transpose(0, 2, 1, 3)
    vh = v.astype(jnp.float32).reshape(B, T, FOX_HEADS, FOX_DH).transpose(0, 2, 1, 3)
    log_f = jax.nn.log_sigmoid(f_logit.astype(jnp.float32) + f_bias.astype(jnp.float32))
    Fc = jnp.cumsum(log_f, axis=1).transpose(0, 2, 1)
    scale = FOX_DH ** -0.5
    outs = []
    for blk in range(T // Q_BLOCK):
        lo, hi = blk * Q_BLOCK, (blk + 1) * Q_BLOCK
        s = (jnp.einsum('bhqd,bhkd->bhqk', qh[:, :, lo:hi], kh[:, :, :hi]) * scale
             + Fc[:, :, lo:hi, None] - Fc[:, :, None, :hi])
        mask = (lo + jnp.arange(Q_BLOCK))[:, None] >= jnp.arange(hi)[None, :]
        p = jax.nn.softmax(jnp.where(mask, s, -jnp.inf), axis=-1)
        outs.append(jnp.einsum('bhqk,bhkd->bhqd', p, vh[:, :, :hi]))
    o = jnp.concatenate(outs, axis=2)
    return o.transpose(0, 2, 1, 3).reshape(B, T, FOX_WIDTH)


def _memory_mixer(q, mem_kv, q_g, k_g):
    B, T, _ = q.shape
    M = mem_kv.shape[1]
    qh = _rmsnorm(q.reshape(B, T, MEM_HEADS, MEM_DH), q_g)
    mk, mv = jnp.split(mem_kv, 2, axis=-1)
    kh = _rmsnorm(mk.reshape(B, M, MEM_HEADS, MEM_DH), k_g)
    vh = mv.astype(jnp.float32).reshape(B, M, MEM_HEADS, MEM_DH)
    s = jnp.einsum('bthd,bmhd->bhtm', qh, kh) * (MEM_DH ** -0.5)
    p = jax.nn.softmax(s, axis=-1)
    return jnp.einsum('bhtm,bmhd->bthd', p, vh).reshape(B, T, MEM_WIDTH)


def _conv_glu_ffn(h, w_up, conv_w, conv_b, w_down):
    T = h.shape[1]
    a, v = jnp.split(h @ w_up.astype(jnp.float32), 2, axis=-1)
    ap = jnp.pad(a, ((0, 0), (CONV_W - 1, 0), (0, 0)))
    a = sum(ap[:, j:j + T] * conv_w[j].astype(jnp.float32) for j in range(CONV_W)) + conv_b.astype(jnp.float32)
    return (jax.nn.gelu(a, approximate=False) * v) @ w_down.astype(jnp.float32)


def setup_inputs(seed: int = 0) -> dict:
    key = jax.random.key(seed)
    ks = jax.random.split(key, 24)
    f32 = jnp.float32
    L = DEPTH

    def nrm(k, shape, fan_in):
        return jax.random.normal(k, shape, f32) * (fan_in ** -0.5)

    def gain(k, shape):
        return 1.0 + 0.02 * jax.random.normal(k, shape, f32)

    return {
        "x": jax.random.normal(ks[0], (BATCH, SEQ, D_MODEL), f32),
        "mem": jax.random.normal(ks[1], (BATCH, N_MEM, D_MODEL), f32),
        "norm_mix_g": gain(ks[2], (L, D_MODEL)),
        "norm_mem_g": gain(ks[3], (L, D_MODEL)),
        "w_in": nrm(ks[4], (L, D_MODEL, IN_COLS), D_MODEL),
        "hgrn_lb_logits": 0.1 * jax.random.normal(ks[5], (L + 1, HG_WIDTH), f32),
        "hgrn_norm_g": gain(ks[6], (L, HG_DV)),
        "fox_f_bias": 1.0 + 0.1 * jax.random.normal(ks[7], (L, FOX_HEADS), f32),
        "fox_q_norm_g": gain(ks[8], (L, FOX_DH)),
        "fox_k_norm_g": gain(ks[9], (L, FOX_DH)),
        "mem_kv_w": nrm(ks[10], (L, D_MODEL, 2 * MEM_WIDTH), D_MODEL),
        "mem_q_norm_g": gain(ks[11], (L, MEM_DH)),
        "mem_k_norm_g": gain(ks[12], (L, MEM_DH)),
        "w_br_hgrn": nrm(ks[13], (L, HG_WIDTH, D_MODEL), HG_WIDTH),
        "w_br_fox": nrm(ks[14], (L, FOX_WIDTH, D_MODEL), FOX_WIDTH),
        "w_br_mem": nrm(ks[15], (L, MEM_WIDTH, D_MODEL), MEM_WIDTH),
        "w_out": nrm(ks[16], (L, D_MODEL, D_MODEL), D_MODEL),
        "norm_ffn_g": gain(ks[17], (L, D_MODEL)),
        "ffn_w_up": nrm(ks[18], (L, D_MODEL, 2 * D_FF), D_MODEL),
        "ffn_conv_w": nrm(ks[19], (L, CONV_W, D_FF), CONV_W),
        "ffn_conv_b": 0.02 * jax.random.normal(ks[20], (L, D_FF), f32),
        "ffn_w_down": nrm(ks[21], (L, D_FF, D_MODEL), D_FF),
    }


def reference(x, mem, norm_mix_g, norm_mem_g, w_in, hgrn_lb_logits, hgrn_norm_g, fox_f_bias,
              fox_q_norm_g, fox_k_norm_g, mem_kv_w, mem_q_norm_g, mem_k_norm_g,
              w_br_hgrn, w_br_fox, w_br_mem, w_out, norm_ffn_g, ffn_w_up, ffn_conv_w,
              ffn_conv_b, ffn_w_down):
    B, T, _ = x.shape
    lower_bounds = jnp.cumsum(jax.nn.softmax(hgrn_lb_logits.astype(jnp.float32), axis=0), axis=0)
    pts = _split_points()
    for l in range(DEPTH):
        h = _rmsnorm(x, norm_mix_g[l])
        z = h @ w_in[l].astype(jnp.float32)
        (hq, hf, hi, hg, fq, fk, fv, ff, mq, gate_logits) = jnp.split(z, pts, axis=-1)
        y_a = _hgrn2_mixer(hq, hf, hi, hg, lower_bounds[l], hgrn_norm_g[l])
        y_b = _fox_mixer(fq, fk, fv, ff, fox_f_bias[l], fox_q_norm_g[l], fox_k_norm_g[l])
        mem_kv = _rmsnorm(mem, norm_mem_g[l]) @ mem_kv_w[l].astype(jnp.float32)
        y_c = _memory_mixer(mq, mem_kv, mem_q_norm_g[l], mem_k_norm_g[l])
        gates = jax.nn.sigmoid(gate_logits).reshape(B, T, N_BRANCH, D_MODEL)
        merged = (gates[:, :, 0] * (y_a @ w_br_hgrn[l].astype(jnp.float32))
                  + gates[:, :, 1] * (y_b @ w_br_fox[l].astype(jnp.float32))
                  + gates[:, :, 2] * (y_c @ w_br_mem[l].astype(jnp.float32)))
        x = x + (merged @ w_out[l].astype(jnp.float32)).astype(x.dtype)
        h2 = _rmsnorm(x, norm_ffn_g[l])
        x = x + _conv_glu_ffn(h2, ffn_w_up[l], ffn_conv_w[l], ffn_conv_b[l], ffn_w_down[l]).astype(x.dtype)
    return x
```

```python
import contextlib
import numpy as np
import concourse.bass as bass
import concourse.mybir as mybir
from concourse.bass_utils import run_bass_kernel_spmd

F32 = mybir.dt.float32
BF16 = mybir.dt.bfloat16
AF = mybir.ActivationFunctionType
ALU = mybir.AluOpType
AX = mybir.AxisListType

D = 1024
DFF = 2816
NCH = DFF // 128
NMEM = 256
EPS = 1e-6
TT = 512
NSUB = 4
NSLOT = 6

BLK_FQ, BLK_FK, BLK_FV = 0, 1, 2
BLK_HQ, BLK_HF, BLK_HI, BLK_HG = 3, 4, 5, 6
BLK_MQ = 7
BLK_GATE0 = 8
BLK_BR0 = 14
BLK_WOUT0 = 17
BLK_UP0 = 19
BLK_DN0 = 30
BLK_MKV0 = 36
NBLK = 38

PO_HNG = 0
PO_FB = 128
PO_FQG, PO_FKG = 136, 200
PO_MQG, PO_MKG = 264, 392
NPAR = 520
GC_MIX, GC_MEM, GC_FFN = 0, 8, 16

CO_MMID, CO_UUP, CO_LINC, CO_HALF, CO_ONES, CO_MASK01, CO_MASKNEG, CO_IDENT = [i * 128 for i in range(8)]
CO_ONEHALF = 8 * 128
NCST = 8 * 128 + 2


class Buf:
    __slots__ = ("name", "t", "last_w", "readers", "aliases")

    def __init__(self, name, t):
        self.name, self.t = name, t
        self.last_w = None
        self.readers = []
        self.aliases = []


class Chan:
    def __init__(self, name, wait_total=False):
        self.name, self.wait_total = name, wait_total
        self.n = 0
        self.sem = None


class Op:
    __slots__ = ("eng", "fn", "deps", "needed", "val", "chan", "cn")

    def __init__(self, eng, fn, chan):
        self.eng, self.fn, self.chan = eng, fn, chan
        self.deps = []
        self.needed = False
        self.val = 0
        self.cn = 0


ENGS = ("pe", "act", "dve", "pool", "sp")


class Prog:
    def __init__(self):
        self.ops = {e: [] for e in ENGS}
        self.chans = []

    def chan(self, name, wait_total=False):
        c = Chan(name, wait_total)
        self.chans.append(c)
        return c

    def op(self, eng, fn, reads=(), writes=(), chan=None):
        o = Op(eng, fn, chan)
        if chan is not None:
            chan.n += 1
            o.cn = chan.n
        deps = {}

        def add(p, raw):
            if p is None:
                return
            if p.chan is None and o.chan is None and p.eng == eng:
                if (not raw) or eng == "pe":
                    return
            deps[id(p)] = p

        writes = list(writes) + [a for b in writes for a in b.aliases]
        for b in reads:
            add(b.last_w, True)
        for b in writes:
            add(b.last_w, False)
            for r in b.readers:
                add(r, False)
        for b in reads:
            b.readers.append(o)
        for b in writes:
            b.last_w = o
            b.readers = []
        o.deps = list(deps.values())
        for p in o.deps:
            p.needed = True
        self.ops[eng].append(o)
        return o

    def finalize(self):
        for e in ENGS:
            c = 0
            for o in self.ops[e]:
                if o.chan is None and o.needed:
                    c += 1
                    o.val = c

    def emit(self, engname, e, sems):
        seen = {}
        for o in self.ops[engname]:
            want = {}
            for p in o.deps:
                if p.chan is not None:
                    key = ("c", id(p.chan))
                    v = 16 * (p.chan.n if p.chan.wait_total else p.cn)
                    s = p.chan.sem
                else:
                    key = ("e", p.eng)
                    v = p.val
                    s = sems[p.eng]
                if v > want.get(key, (0, None))[0]:
                    want[key] = (v, s)
            for key, (v, s) in want.items():
                if seen.get(key, 0) >= v:
                    continue
                e.wait_ge(s, v)
                seen[key] = v
            ins = o.fn(e)
            if o.chan is not None:
                ins.then_inc(o.chan.sem, 16)
            elif o.needed:
                ins.then_inc(sems[engname], 1)


def _host_consts():
    s1 = np.arange(128)[:, None]
    s0 = np.arange(128)[None, :]
    c = np.zeros((128, NCST), np.float32)
    mmid = np.where((s1 > 63) & (s1 <= s0), 1.0, 0.0) - np.where((s1 <= 63) & (s1 > s0), 1.0, 0.0)
    c[:, CO_MMID:CO_MMID + 128] = mmid
    c[:, CO_UUP:CO_UUP + 128] = (s1 > s0)
    c[:, CO_LINC:CO_LINC + 128] = (s1 <= s0)
    c[:, CO_HALF:CO_HALF + 128] = (s1 <= 63) * np.ones((1, 128))
    c[:, CO_ONES:CO_ONES + 128] = 1.0
    c[:, CO_MASK01:CO_MASK01 + 128] = (s1 <= s0)
    c[:, CO_MASKNEG:CO_MASKNEG + 128] = np.where(s1 <= s0, 0.0, -30000.0)
    c[:, CO_IDENT:CO_IDENT + 128] = np.eye(128)
    c[:, CO_ONEHALF] = 1.0
    c[:, CO_ONEHALF + 1] = (np.arange(128) <= 63)
    return c


def _host_wblocks(w_in, w_br_hgrn, w_br_fox, w_br_mem, w_out, ffn_w_up, ffn_w_down, mem_kv_w):
    wb = np.zeros((NBLK, 128, 4096), np.float32)

    def k8(cols):
        return cols.reshape(8, 128, 512).transpose(1, 0, 2).reshape(128, 4096)

    col = {BLK_HQ: 0, BLK_HF: 512, BLK_HI: 1024, BLK_HG: 1536, BLK_FQ: 2048, BLK_FK: 2560,
           BLK_FV: 3072, BLK_MQ: 3592}
    for b, c0 in col.items():
        wb[b] = k8(w_in[:, c0:c0 + 512])
    g = w_in[:, 4104:4104 + 3072].reshape(1024, 3, 8, 128).transpose(0, 2, 1, 3).reshape(1024, 3072)
    for i in range(6):
        wb[BLK_GATE0 + i] = k8(g[:, i * 512:(i + 1) * 512])
    for i, w in enumerate((w_br_hgrn, w_br_fox, w_br_mem)):
        wb[BLK_BR0 + i] = w.reshape(4, 128, 1024).transpose(1, 0, 2).reshape(128, 4096)
    for i in range(2):
        wb[BLK_WOUT0 + i] = k8(w_out[:, i * 512:(i + 1) * 512])
    a, v = ffn_w_up[:, :DFF], ffn_w_up[:, DFF:]
    for u in range(11):
        cols = np.concatenate([a[:, (2 * u) * 128:(2 * u + 1) * 128], v[:, (2 * u) * 128:(2 * u + 1) * 128],
                               a[:, (2 * u + 1) * 128:(2 * u + 2) * 128], v[:, (2 * u + 1) * 128:(2 * u + 2) * 128]], axis=1)
        wb[BLK_UP0 + u] = k8(cols)
    wd = ffn_w_down.reshape(NCH, 128, 1024)
    for cb in range(2):
        for kb in range(3):
            chs = list(range(kb * 8, min(kb * 8 + 8, NCH)))
            blk = np.zeros((128, 8, 512), np.float32)
            for n, ch in enumerate(chs):
                blk[:, n, :] = wd[ch][:, cb * 512:(cb + 1) * 512]
            wb[BLK_DN0 + cb * 3 + kb] = blk.reshape(128, 4096)
    for i in range(2):
        wb[BLK_MKV0 + i] = k8(mem_kv_w[:, i * 512:(i + 1) * 512])
    return wb


def build_nc(nseq, T):
    ntile = T // TT
    nqb = T // 128
    nc = bass.Bass("TRN2", target_bir_lowering=False)
    x_d = nc.dram_tensor("x", [nseq, T, D], F32, kind="ExternalInput").ap()
    mem_d = nc.dram_tensor("mem", [nseq, NMEM, D], F32, kind="ExternalInput").ap()
    wb_d = nc.dram_tensor("wblocks", [NBLK, 128, 4096], F32, kind="ExternalInput").ap()
    par_d = nc.dram_tensor("params", [1, NPAR], F32, kind="ExternalInput").ap()
    cst_d = nc.dram_tensor("consts", [128, NCST], F32, kind="ExternalInput").ap()
    cvw_d = nc.dram_tensor("convp", [128, NCH * 4], F32, kind="ExternalInput").ap()
    wff_d = nc.dram_tensor("wff", [128, 64], F32, kind="ExternalInput").ap()
    gcol_d = nc.dram_tensor("gcol", [128, 24], F32, kind="ExternalInput").ap()
    lbl_d = nc.dram_tensor("lblog", [2, 512], F32, kind="ExternalInput").ap()
    y_d = nc.dram_tensor("y", [nseq, T, D], F32, kind="ExternalOutput").ap()
    wbf_d = nc.dram_tensor("wbf", [NBLK, 128, 4096], BF16, kind="Internal").ap()

    P = Prog()
    es = contextlib.ExitStack()

    def sb(name, shape, dt=F32):
        return Buf(name, es.enter_context(nc.sbuf_tensor(name, shape, dt)))

    def dram_buf(name):
        return Buf(name, None)

    with es:
        par = sb("par", [128, NPAR])
        cst = sb("cst", [128, NCST])
        cvp = sb("cvp", [128, NCH * 4])
        wff32 = sb("wff32", [128, 64])
        wffb = sb("wffb", [128, 8, 8], BF16)
        ident = sb("ident", [128, 128], BF16)
        maskneg = sb("maskneg", [128, 128], BF16)
        oml = sb("oml", [128, 512])
        gqs = sb("gqs", [128, 64])
        gms = sb("gms", [128, 128])
        gcol = sb("gcol_sb", [128, 24])
        xres = [sb(f"xres{i}", [128, NSUB, D]) for i in range(2)]
        hbf = [sb(f"hbf{i}", [128, D], BF16) for i in range(3)]
        hT = sb("hT", [128, 8, TT], BF16)
        U_t = es.enter_context(nc.sbuf_tensor("Ureg", [128, 12288], BF16))
        stat = [sb(f"stat{i}", [128, 24]) for i in range(4)]
        wslot = [sb(f"wslot{i}", [128, 4096], BF16) for i in range(NSLOT)]
        QT = Buf("QT", U_t[:, 0:2048].rearrange("p (a b) -> p a b", a=4))
        KT = sb("KT", [128, 4, T], BF16)
        Vaug = sb("Vaug", [128, nqb, 8, 66], BF16)
        negFc = sb("negFc", [128, nqb, 8])
        nCb = sb("nCb", [128, nqb, 8])
        ncarry = sb("ncarry", [128, 8])
        lfb = [sb(f"lfb{i}", [128, 8]) for i in range(2)]
        biasT = [sb(f"biasT{i}", [128, 8]) for i in range(4)]
        wtb = [sb(f"wtb{i}", [128, 8]) for i in range(4)]
        Vs = [sb(f"Vs{i}", [128, 8, 66], BF16) for i in range(3)]
        PT = [sb(f"PT{i}", [128, 512], BF16) for i in range(3)]
        t32 = [sb(f"t32_{i}", [128, 512]) for i in range(6)]
        tb16 = [sb(f"tb16_{i}", [128, 512], BF16) for i in range(4)]
        rinv = [sb(f"rinv{i}", [128, 8]) for i in range(2)]
        yT = [Buf(f"yT{i}", U_t[:, 2048 * (i + 1):2048 * (i + 2)].rearrange("p (a b) -> p a b", a=4))
              for i in range(3)]
        sq = [sb("sq0", [128, 512], BF16)] * NSUB
        kk = [sb("kk0", [128, 512])] * NSUB
        lgf = [sb("lgf0", [128, 512])] * NSUB
        ivb = [sb("ivb0", [128, 512], BF16)] * NSUB
        sgn = [sb("sgn0", [128, 512], BF16)] * NSUB
        Sst = sb("Sst", [128, 512])
        Sbf = sb("Sbf", [128, 512], BF16)
        dece = [sb(f"dece{i}", [128, 8]) for i in range(2)]
        qtT = sb("qtT", [128, 4, 128], BF16)
        ktT = sb("ktT", [128, 4, 128], BF16)
        ATm = sb("ATm", [128, 4, 128], BF16)
        KmT = sb("KmT", [128, 4, NMEM], BF16)
        Vm = sb("Vm", [128, 2, 4, 130], BF16)
        mqT = sb("mqT", [128, 4, 128], BF16)
        mergedT = Buf("mergedT", U_t[:, 8192:12288].rearrange("p (a b) -> p a b", a=8))
        abuf = [sb(f"abuf{i}", [128, TT + 2]) for i in range(2)]
        aprev = sb("aprev", [128, NCH, 2])
        ybuf = Buf("ybuf", U_t[:, 0:NCH * TT].rearrange("p (a b) -> p a b", a=NCH))
        QTs = [Buf(f"QT{i}", QT.t) for i in range(NSUB)]
        ybg = [Buf(f"ybuf_g{i}", ybuf.t) for i in range(3)]
        mix_bufs = [QT, mergedT] + yT + QTs
        for b_ in mix_bufs:
            b_.aliases = [ybuf] + ybg
        ybuf.aliases = list(mix_bufs)
        for b_ in ybg:
            b_.aliases = list(mix_bufs)

        banks = [Buf(f"bank{i}", es.enter_context(nc.psum_tensor(f"bank{i}", [128, 512], F32))) for i in range(8)]
        free_banks = list(range(8))

        def acq():
            assert free_banks, "out of PSUM banks"
            return banks[free_banks.pop(0)]

        def rel(b):
            free_banks.append(banks.index(b))

        def bf(bank):
            return bank.t[:].bitcast(BF16)

        wbf_blk = [dram_buf(f"wbf{i}") for i in range(NBLK)]
        cast_groups = [[BLK_MKV0], [BLK_MKV0 + 1], [BLK_FQ], [BLK_FK, BLK_FV],
                       [BLK_HQ, BLK_HF, BLK_HI, BLK_HG, BLK_MQ],
                       [BLK_BR0 + i for i in range(3)] + [BLK_GATE0 + i for i in range(6)],
                       [BLK_WOUT0, BLK_WOUT0 + 1] + [BLK_UP0 + i for i in range(11)],
                       [BLK_DN0 + i for i in range(6)]]
        ch_cast = [P.chan(f"cast{i}", True) for i in range(len(cast_groups))]
        ch_setup = P.chan("setup", True)
        ch_slot = [P.chan(f"slot{i}") for i in range(NSLOT)]
        ch_x = [P.chan(f"x{i}") for i in range(2)]
        ch_out = [P.chan(f"out{i}") for i in range(2)]
        ch_mem = P.chan("mem")

        def mm(out, lhsT, rhs, start, stop, reads, writes, skip=False):
            if skip:
                P.op("pe", lambda e: e.matmul(out, lhsT=lhsT, rhs=rhs, start=start, stop=stop, skip_group_check=True),
                     reads, writes)
            else:
                P.op("pe", lambda e: e.matmul(out, lhsT=lhsT, rhs=rhs, start=start, stop=stop), reads, writes)

        def tr(out, in_, reads, writes):
            P.op("pe", lambda e: e.transpose(out, in_, ident.t[:]), list(reads) + [ident], writes)

        def act(out, in_, func, reads, writes, scale=1.0, bias=0.0, accum=None):
            if accum is None:
                P.op("act", lambda e: e.activation(out=out, in_=in_, func=func, bias=bias, scale=scale), reads, writes)
            else:
                P.op("act", lambda e: e.activation(out=out, in_=in_, func=func, bias=bias, scale=scale,
                                                   accum_out=accum), reads, writes)

        def tt(eng, out, in0, in1, op, reads, writes):
            P.op(eng, lambda e: e.tensor_tensor(out=out, in0=in0, in1=in1, op=op), reads, writes)

        def cp(eng, out, in_, reads, writes):
            if eng == "act":
                P.op("act", lambda e: e.copy(out=out, in_=in_), reads, writes)
            else:
                P.op(eng, lambda e: e.tensor_copy(out=out, in_=in_), reads, writes)

        def ts(eng, out, in0, s1, s2, op0, op1, reads, writes):
            P.op(eng, lambda e: e.tensor_scalar(out=out, in0=in0, scalar1=s1, scalar2=s2, op0=op0, op1=op1),
                 reads, writes)

        def stt(eng, out, in0, scalar, in1, op0, op1, reads, writes):
            P.op(eng, lambda e: e.scalar_tensor_tensor(out=out, in0=in0, scalar=scalar, in1=in1, op0=op0, op1=op1),
                 reads, writes)

        def rsum(out, in_, reads, writes):
            P.op("dve", lambda e: e.reduce_sum(out=out, in_=in_, axis=AX.X), reads, writes)

        def dma(eng, out, in_, chan, reads, writes):
            P.op(eng, lambda e: e.dma_start(out=out, in_=in_), reads, writes, chan=chan)

        stat_i = [0]

        def next_stat():
            stat_i[0] = (stat_i[0] + 1) % 4
            return stat[stat_i[0]]

        def mkpool(name, n=3):
            bufs = [sb(f"{name}{i}", [128, 24]) for i in range(n)]
            idx = [0]

            def nxt():
                idx[0] = (idx[0] + 1) % n
                return bufs[idx[0]]
            return nxt

        f_stat, h_stat, m_stat = mkpool("fst"), mkpool("hst"), mkpool("mst")
        def uview(k):
            b_ = Buf(f"uv{k}", U_t[:, 8192 + k * 512:8192 + (k + 1) * 512])
            b_.aliases = [mergedT, ybuf] + ybg
            mergedT.aliases.append(b_)
            ybuf.aliases.append(b_)
            for g_ in ybg:
                g_.aliases.append(b_)
            return b_

        f_tq, f_tb, m_tq, m_tb, m_PT = uview(0), [uview(1), uview(2)], uview(3), [uview(4), uview(5)], [uview(6), uview(7)]
        f_tn = Buf("f_tn", abuf[0].t[:, 0:512]); f_tn.aliases = [abuf[0]]; abuf[0].aliases = [f_tn]
        m_tn = Buf("m_tn", abuf[1].t[:, 0:512]); m_tn.aliases = [abuf[1]]; abuf[1].aliases = [m_tn]
        m_rinv = [sb(f"mrinv{i}", [128, 8]) for i in range(2)]

        def rstd_from_ss(st, n, width):
            act(st.t[:, 8:8 + width], st.t[:, 0:width], AF.Ln, [st], [st], scale=1.0 / n, bias=EPS)
            act(st.t[:, 16:16 + width], st.t[:, 8:8 + width], AF.Exp, [st], [st], scale=-0.5)
            return st.t[:, 16:16 + width]

        def bc_mid(ap2, n):
            return ap2.unsqueeze(1).to_broadcast([128, n, ap2.shape[1]])

        def bc_last(ap2, n):
            return ap2.unsqueeze(2).to_broadcast([128, ap2.shape[1], n])

        def v3(ap2, a):
            return ap2.rearrange("p (a b) -> p a b", a=a)

        free_slots = list(range(NSLOT))

        def wload(blk):
            assert free_slots, "out of weight slots"
            k = free_slots.pop(0)
            s = wslot[k]
            dma("sp", s.t[:], wbf_d[blk], ch_slot[k], [wbf_blk[blk]], [s])
            return s

        def wrel(s):
            free_slots.append(wslot.index(s))

        hbf_i = [0]

        def norm_p1(src_ap, src_buf):
            st = next_stat()
            hbf_i[0] = (hbf_i[0] + 1) % 3
            hb = hbf[hbf_i[0]]
            act(hb.t[:], src_ap, AF.Square, [src_buf], [hb, st], accum=st.t[:, 0:1])
            r = rstd_from_ss(st, D, 1)
            ts("dve", hb.t[:], src_ap, r, None, ALU.mult, ALU.bypass, [src_buf, st], [hb])
            return hb

        def norm_transpose(src_ap, src_buf, gc0, dstT, col0):
            norm_p2(norm_p1(src_ap, src_buf), gc0, dstT, col0)

        def norm_p2(hb, gc0, dstT, col0):
            pb = acq()
            for kc in range(8):
                tr(bf(pb)[:, kc * 128:(kc + 1) * 128], hb.t[:, kc * 128:(kc + 1) * 128], [hb], [pb])
            tt("dve", dstT.t[:, :, col0:col0 + 128], v3(bf(pb), 8), bc_last(gcol.t[:, gc0:gc0 + 8], 128), ALU.mult,
               [pb, gcol], [dstT])
            rel(pb)

        def headnorm_to_bf(pb, nh, g_ap, gbuf, out_bf, tq=None, tn=None, statf=None):
            headnorm_b(headnorm_a(pb, nh, tq, statf), pb, nh, g_ap, gbuf, out_bf, tn)

        def headnorm_a(pb, nh, tq=None, statf=None):
            st = (statf or next_stat)()
            tq = tq or t32[0]
            act(tq.t[:], pb.t[:], AF.Square, [pb], [tq])
            rsum(st.t[:, 0:nh], v3(tq.t[:], nh), [tq], [st])
            return st

        def headnorm_b(st, pb, nh, g_ap, gbuf, out_bf, tn=None):
            hd = 512 // nh
            r = rstd_from_ss(st, hd, nh)
            tn = tn or t32[1]
            tt("dve", v3(tn.t[:], nh), v3(pb.t[:], nh), bc_last(r, hd), ALU.mult, [pb, st], [tn])
            tt("pool", v3(out_bf.t[:], nh), v3(tn.t[:], nh), bc_mid(g_ap, nh), ALU.mult, [tn, gbuf], [out_bf])

        def transpose4(src_bf, dstT, col0):
            pb = acq()
            for c in range(4):
                tr(bf(pb)[:, c * 128:(c + 1) * 128], src_bf.t[:, c * 128:(c + 1) * 128], [src_bf], [pb])
            cp("dve", dstT.t[:, :, col0:col0 + 128], v3(bf(pb)[:, 0:512], 4), [pb], [dstT])
            rel(pb)

        def proj_tok(ws, s, nk=8):
            pb = acq()
            for kc in range(nk):
                mm(pb.t[:], hT.t[:, kc, s * 128:(s + 1) * 128], ws.t[:, kc * 512:(kc + 1) * 512],
                   kc == 0, kc == nk - 1, [hT, ws], [pb])
            return pb

        def issue_casts(grps):
            for grp in grps:
                for b in cast_groups[grp]:
                    dma("pool", wbf_d[b], wb_d[b], ch_cast[grp], [], [wbf_blk[b]])

        issue_casts(range(0, 5))
        dma("sp", par.t[:], par_d.partition_broadcast(128), ch_setup, [], [par])
        dma("sp", gcol.t[:], gcol_d, ch_setup, [], [gcol])
        dma("sp", t32[0].t[:], lbl_d[0:1, :].partition_broadcast(128), ch_setup, [], [t32[0]])
        dma("sp", t32[1].t[:], lbl_d[1:2, :].partition_broadcast(128), ch_setup, [], [t32[1]])
        dma("sp", cst.t[:], cst_d, ch_setup, [], [cst])
        dma("sp", cvp.t[:], cvw_d, ch_setup, [], [cvp])
        dma("sp", wff32.t[:], wff_d, ch_setup, [], [wff32])
        cp("dve", ident.t[:], cst.t[:, CO_IDENT:CO_IDENT + 128], [cst], [ident])
        cp("dve", maskneg.t[:], cst.t[:, CO_MASKNEG:CO_MASKNEG + 128], [cst], [maskneg])
        cp("dve", wffb.t[:].rearrange("p a b -> p (a b)"), wff32.t[:], [wff32], [wffb])
        tt("dve", oml.t[:], t32[1].t[:], t32[0].t[:], ALU.subtract, [t32[0], t32[1]], [oml])
        act(oml.t[:], oml.t[:], AF.Sigmoid, [oml], [oml])
        ts("dve", gqs.t[:], par.t[:, PO_FQG:PO_FQG + 64], 0.125, None, ALU.mult, ALU.bypass, [par], [gqs])
        ts("dve", gms.t[:], par.t[:, PO_MQG:PO_MQG + 128], float(128 ** -0.5), None, ALU.mult, ALU.bypass, [par], [gms])
        P.op("pool", lambda e: e.memset(Vaug.t[:], 1.0), [], [Vaug])
        P.op("pool", lambda e: e.memset(Vm.t[:], 1.0), [], [Vm])

        hng = par.t[:, PO_HNG:PO_HNG + 128]
        fkg = par.t[:, PO_FKG:PO_FKG + 64]
        mkg = par.t[:, PO_MKG:PO_MKG + 128]
        fbias = par.t[:, PO_FB:PO_FB + 8]
        mask01 = cst.t[:, CO_MASK01:CO_MASK01 + 128]

        import os as _os
        _stop = _os.environ.get("KSTOP", "")

        class _Stop(Exception):
            pass

        def ck(name):
            if _stop == name:
                raise _Stop()

        gt = 0
        try:
          ck("setup")
          tiles = [(b_, ti_) for b_ in range(nseq) for ti_ in range(ntile)]

          def issue_loads(idx):
              b, ti = tiles[idx]
              xr = xres[idx % 2]
              if ti == 0:
                  dma("sp", xr.t[:, 0:2, :], mem_d[b].rearrange("(s p) d -> p s d", p=128), ch_mem, [], [xr])
              else:
                  dma("sp", xr.t[:], x_d[b, ti * TT:(ti + 1) * TT, :].rearrange("(s p) d -> p s d", p=128),
                      ch_x[idx % 2], [], [xr])

          def gen_prologue(idx):
              b, ti = tiles[idx]
              xr = xres[idx % 2]
              if ti == 0:
                  memx, memhT = xr, hT
                  hbs = []
                  for s in range(2):
                      hbs.append(norm_p1(memx.t[:, s, :], memx))
                      yield
                  dma("sp", xr.t[:], x_d[b, 0:TT, :].rearrange("(s p) d -> p s d", p=128), ch_x[idx % 2], [], [xr])
                  for s in range(2):
                      norm_p2(hbs[s], GC_MEM, memhT, s * 128)
                      yield
                  wk = wload(BLK_MKV0)
                  for s in range(2):
                      pb = acq()
                      for kc in range(8):
                          mm(pb.t[:], memhT.t[:, kc, s * 128:(s + 1) * 128], wk.t[:, kc * 512:(kc + 1) * 512],
                             kc == 0, kc == 7, [memhT, wk], [pb])
                      yield
                      headnorm_to_bf(pb, 4, mkg, par, tb16[0])
                      rel(pb)
                      yield
                      transpose4(tb16[0], KmT, s * 128)
                      yield
                  wrel(wk)
                  wv = wload(BLK_MKV0 + 1)
                  for s in range(2):
                      pb = acq()
                      for kc in range(8):
                          mm(pb.t[:], memhT.t[:, kc, s * 128:(s + 1) * 128], wv.t[:, kc * 512:(kc + 1) * 512],
                             kc == 0, kc == 7, [memhT, wv], [pb])
                      cp("act", Vm.t[:, s, :, 0:128], v3(pb.t[:], 4), [pb], [Vm])
                      rel(pb)
                      yield
                  wrel(wv)
                  P.op("pool", lambda e: e.memset(Sst.t[:], 0.0), [], [Sst])
                  P.op("pool", lambda e: e.memset(aprev.t[:], 0.0), [], [aprev])
                  P.op("pool", lambda e: e.memset(ncarry.t[:], 0.0), [], [ncarry])
              hbs = []
              for s in range(NSUB):
                  if len(hbs) == 2:
                      norm_p2(hbs.pop(0), GC_MIX, hT, (s - 2) * 128)
                      yield
                  hbs.append(norm_p1(xr.t[:, s, :], xr))
                  yield
              for s in (NSUB - 2, NSUB - 1):
                  norm_p2(hbs.pop(0), GC_MIX, hT, s * 128)
                  yield
              t0 = ti * TT
              ws = wload(BLK_FK)
              wsv = wload(BLK_FV)
              pbs = {}
              for step in range(NSUB + 3):
                  if step < NSUB:
                      pbs[step] = proj_tok(ws, step)
                      yield
                  if 0 <= step - 1 < NSUB:
                      s1 = step - 1
                      hst = headnorm_a(pbs[s1], 8, None, f_stat)
                      yield
                      headnorm_b(hst, pbs[s1], 8, fkg, par, tb16[s1 % 4], None)
                      rel(pbs.pop(s1))
                      yield
                  if 0 <= step - 2 < NSUB:
                      s2 = step - 2
                      pb = proj_tok(wsv, s2)
                      cp("act", Vaug.t[:, ti * NSUB + s2, :, 0:64], v3(pb.t[:], 8), [pb], [Vaug])
                      rel(pb)
                      yield
                  if 0 <= step - 3 < NSUB:
                      s3 = step - 3
                      transpose4(tb16[s3 % 4], KT, t0 + s3 * 128)
                      yield
              wrel(ws)
              wrel(wsv)
              for s in range(NSUB):
                  qi = ti * NSUB + s
                  pb = acq()
                  for kc in range(8):
                      mm(pb.t[:, 0:8], hT.t[:, kc, s * 128:(s + 1) * 128], wffb.t[:, kc, :], kc == 0, kc == 7,
                         [hT, wffb], [pb])
                  lf = lfb[s % 2]
                  tt("dve", lf.t[:], pb.t[:, 0:8], fbias, ALU.add, [pb, par], [lf])
                  act(lf.t[:], lf.t[:], AF.Exp, [lf], [lf], scale=-1.0)
                  act(lf.t[:], lf.t[:], AF.Ln, [lf], [lf], bias=1.0)
                  yield
                  mm(pb.t[:, 16:24], cst.t[:, CO_LINC:CO_LINC + 128], lf.t[:], True, True, [cst, lf], [pb])
                  mm(pb.t[:, 24:32], cst.t[:, CO_HALF:CO_HALF + 128], lf.t[:], True, True, [cst, lf], [pb])
                  mm(pb.t[:, 32:40], cst.t[:, CO_ONES:CO_ONES + 128], lf.t[:], True, True, [cst, lf], [pb])
                  tt("dve", negFc.t[:, qi, :], pb.t[:, 16:24], ncarry.t[:], ALU.add, [pb, ncarry], [negFc])
                  tt("dve", nCb.t[:, qi, :], pb.t[:, 24:32], ncarry.t[:], ALU.add, [pb, ncarry], [nCb])
                  tt("dve", ncarry.t[:], pb.t[:, 32:40], ncarry.t[:], ALU.add, [pb, ncarry], [ncarry])
                  rel(pb)
                  yield

          def run_gens(gens, weights=None):
              gens = list(gens)
              weights = list(weights or [1] * len(gens))
              while gens:
                  for g_ in list(gens):
                      w_ = weights[gens.index(g_)]
                      for _ in range(w_):
                          try:
                              next(g_)
                          except StopIteration:
                              k_ = gens.index(g_)
                              gens.pop(k_)
                              weights.pop(k_)
                              break

          issue_loads(0)
          run_gens([gen_prologue(0)])
          for idx, (b, ti) in enumerate(tiles):
              if True:
                  gt = idx
                  xr = xres[idx % 2]
                  t0 = ti * TT
                  ck('A')
                  def gen_fox(ti=ti, t0=t0):
                      ws = wload(BLK_FQ)
                      qstage = {}

                      def q_stages(s):
                          st_ = {}

                          def a():
                              st_["pb"] = proj_tok(ws, s)

                          def b():
                              st_["st"] = headnorm_a(st_["pb"], 8, f_tq, f_stat)

                          def b2():
                              headnorm_b(st_["st"], st_["pb"], 8, gqs.t[:], gqs, f_tb[s % 2], f_tn)
                              rel(st_["pb"])

                          def c():
                              transpose4(f_tb[s % 2], QTs[s], s * 128)
                          return [a, b, b2, c]

                      for f_ in q_stages(0):
                          f_()
                          yield
                      pend_q = []
                      steps = []
                      for s in range(NSUB):
                          for j in range(ti * NSUB + s + 1):
                              for hh in range(2):
                                  steps.append((s, j, hh))
                      st_sbk, st_pt, st_vs = {}, {}, {}
                      obs_ = {}

                      def att_prep(k):
                          s, j, hh = steps[k]
                          qi = ti * NSUB + s
                          if hh == 0:
                              n = k // 2
                              bt, wt_, vs = biasT[n % 4], wtb[n % 4], Vs[n % 3]
                              tt("dve", bt.t[:], negFc.t[:, j, :], nCb.t[:, qi, :], ALU.subtract, [negFc, nCb], [bt])
                              act(wt_.t[:], bt.t[:], AF.Exp, [bt], [wt_])
                              tt("pool" if n % 2 else "dve", vs.t[:, :, 0:65], Vaug.t[:, j, :, 0:65], bc_last(wt_.t[:], 65),
                                 ALU.mult, [Vaug, wt_], [vs])
                              st_vs[(s, j)] = vs

                      def att_st(k):
                          s, j, hh = steps[k]
                          qi = ti * NSUB + s
                          sbk = acq()
                          for h4 in range(4):
                              r0 = hh * 64
                              mm(sbk.t[:, h4 * 128:(h4 + 1) * 128], KT.t[r0:r0 + 64, h4, j * 128:(j + 1) * 128],
                                 QT.t[r0:r0 + 64, h4, s * 128:(s + 1) * 128], True, j != qi, [KT, QTs[s]], [sbk])
                              if j == qi:
                                  mm(sbk.t[:, h4 * 128:(h4 + 1) * 128], ident.t[:], maskneg.t[:], False, True,
                                     [ident, maskneg], [sbk])
                          st_sbk[k] = sbk

                      def att_exp(k):
                          sbk = st_sbk.pop(k)
                          pt = PT[k % 3]
                          act(pt.t[:], sbk.t[:], AF.Exp, [sbk], [pt])
                          rel(sbk)
                          st_pt[k] = pt

                      def att_pv(k):
                          s, j, hh = steps[k]
                          qi = ti * NSUB + s
                          pt = st_pt.pop(k)
                          vs = st_vs[(s, j)]
                          if (s, hh) not in obs_:
                              obs_[(s, hh)] = acq()
                          ob_ = obs_[(s, hh)]
                          for h4 in range(4):
                              h = h4 * 2 + hh
                              mm(ob_.t[:, h4 * 128:h4 * 128 + 65], pt.t[:, h4 * 128:(h4 + 1) * 128],
                                 vs.t[:, h, 0:65], j == 0 and h4 == 0, False, [pt, vs], [ob_], skip=True)
                          if j == qi and hh == 1:
                              yb = f_tb[s % 2]
                              for h2 in range(2):
                                  ob2 = obs_.pop((s, h2))
                                  ri = rinv[h2]
                                  o3 = ob2.t[:].rearrange("p (h d) -> p h d", h=4)
                                  P.op("dve", lambda e, ri=ri, o3=o3: e.reciprocal(out=ri.t[:, 0:4].unsqueeze(2),
                                                                                   in_=o3[:, :, 64:65]), [ob2], [ri])
                                  yb4 = yb.t[:].rearrange("p (a b d) -> p a b d", a=4, b=2)[:, :, h2, :]
                                  tt("dve", yb4, o3[:, :, 0:64], bc_last(ri.t[:, 0:4], 64), ALU.mult, [ob2, ri], [yb])
                                  rel(ob2)
                              transpose4(yb, yT[1], s * 128)

                      LAG = 2
                      q_done = {0}

                      def need_q(k):
                          s_ = steps[k][0]
                          if s_ not in q_done:
                              while pend_q:
                                  pend_q.pop(0)()
                              q_done.add(s_)

                      for k0 in range(min(LAG, len(steps))):
                          need_q(k0)
                          att_prep(k0)
                          att_st(k0)
                      TRAIL = 2
                      for k in range(len(steps) + TRAIL):
                          if k < len(steps):
                              s_k = steps[k][0]
                              if s_k + 1 < NSUB and (s_k + 1) not in qstage:
                                  qstage[s_k + 1] = True
                                  pend_q.extend(q_stages(s_k + 1))
                          if k + LAG < len(steps):
                              need_q(k + LAG)
                              att_prep(k + LAG)
                              att_st(k + LAG)
                          if k < len(steps):
                              att_exp(k)
                          if k >= TRAIL:
                              att_pv(k - TRAIL)
                          if pend_q and k % 2 == 1:
                              pend_q.pop(0)()
                          yield
                      wrel(ws)

                  def gen_hgrn():
                      wq_, wf_, wi_, wg_ = wload(BLK_HQ), wload(BLK_HF), wload(BLK_HI), wload(BLK_HG)
                      for s in range(NSUB):
                          def sigm(pb_, sc, dst):
                              act(dst.t[:], pb_.t[:], AF.Exp, [pb_], [dst], scale=sc)
                              yield
                              act(dst.t[:], dst.t[:], AF.Ln, [dst], [dst], bias=1.0)
                              yield
                              act(dst.t[:], dst.t[:], AF.Exp, [dst], [dst], scale=-1.0)

                          pb = proj_tok(wq_, s)
                          yield
                          yield from sigm(pb, -1.0, t32[2])
                          tt("dve", sq[s].t[:], pb.t[:], t32[2].t[:], ALU.mult, [pb, t32[2]], [sq[s]])
                          rel(pb)
                          yield
                          pb = proj_tok(wf_, s)
                          yield
                          yield from sigm(pb, 1.0, t32[2])
                          rel(pb)
                          tt("dve", kk[s].t[:], t32[2].t[:], oml.t[:], ALU.mult, [t32[2], oml], [kk[s]])
                          act(lgf[s].t[:], kk[s].t[:], AF.Ln, [kk[s]], [lgf[s]], scale=-1.0, bias=1.0)
                          yield
                          pb = proj_tok(wi_, s)
                          cp("dve", ivb[s].t[:], pb.t[:], [pb], [ivb[s]])
                          rel(pb)
                          yield
                          pb = proj_tok(wg_, s)
                          yield
                          yield from sigm(pb, -1.0, t32[2])
                          tt("dve", t32[2].t[:], pb.t[:], t32[2].t[:], ALU.mult, [pb, t32[2]], [t32[2]])
                          rel(pb)
                          tt("pool", v3(sgn[s].t[:], 4), v3(t32[2].t[:], 4), bc_mid(hng, 4), ALU.mult, [t32[2], par], [sgn[s]])
                          yield
                          e1, e2, e3 = t32[3], t32[4], t32[5]
                          gm = acq()
                          mm(gm.t[:], cst.t[:, CO_MMID:CO_MMID + 128], lgf[s].t[:], True, True, [cst, lgf[s]], [gm])
                          yield
                          act(e1.t[:], gm.t[:], AF.Exp, [gm], [e1])
                          yield
                          act(e2.t[:], gm.t[:], AF.Exp, [gm], [e2], scale=-1.0)
                          rel(gm)
                          yield
                          gl = acq()
                          mm(gl.t[:], cst.t[:, CO_UUP:CO_UUP + 128], lgf[s].t[:], True, True, [cst, lgf[s]], [gl])
                          yield
                          act(e3.t[:], gl.t[:], AF.Exp, [gl], [e3])
                          rel(gl)
                          yield
                          gtb = acq()
                          for h in range(4):
                              mm(gtb.t[:, 2 * h:2 * h + 2], lgf[s].t[:, h * 128:(h + 1) * 128],
                                 cst.t[:, CO_ONEHALF:CO_ONEHALF + 2], True, True, [cst, lgf[s]], [gtb])
                          de = dece[s % 2]
                          act(de.t[:], gtb.t[:, 0:8], AF.Exp, [gtb], [de])
                          rel(gtb)
                          yield
                          qt_, kt_, kp_ = tb16[2], tb16[3], tb16[0]
                          tt("dve", qt_.t[:], sq[s].t[:], e1.t[:], ALU.mult, [sq[s], e1], [qt_])
                          tt("dve", kt_.t[:], kk[s].t[:], e2.t[:], ALU.mult, [kk[s], e2], [kt_])
                          tt("pool", kp_.t[:], kk[s].t[:], e3.t[:], ALU.mult, [kk[s], e3], [kp_])
                          yield
                          transpose4(qt_, qtT, 0)
                          transpose4(kt_, ktT, 0)
                          de3 = de.t[:].rearrange("p (h t) -> p h t", h=4)
                          tt("dve", v3(Sbf.t[:], 4), v3(Sst.t[:], 4), de3[:, :, 1:2].to_broadcast([128, 4, 128]), ALU.mult,
                             [Sst, de], [Sbf])
                          yield
                          atb = acq()
                          for h in range(4):
                              mm(atb.t[:, h * 128:(h + 1) * 128], ktT.t[:, h, :], qtT.t[:, h, :], True, True, [ktT, qtT], [atb])
                          tt("dve", ATm.t[:], v3(atb.t[:], 4), bc_mid(mask01, 4), ALU.mult, [atb, cst], [ATm])
                          rel(atb)
                          yield
                          sup = acq()
                          for h in range(4):
                              hs = slice(h * 128, (h + 1) * 128)
                              mm(sup.t[:, hs], kp_.t[:, hs], ivb[s].t[:, hs], True, True, [kp_, ivb[s]], [sup])
                          tt("dve", v3(Sst.t[:], 4), v3(Sst.t[:], 4), de3[:, :, 0:1].to_broadcast([128, 4, 128]), ALU.mult,
                             [Sst, de], [Sst])
                          tt("dve", Sst.t[:], Sst.t[:], sup.t[:], ALU.add, [Sst, sup], [Sst])
                          rel(sup)
                          yield
                          obk = acq()
                          for h in range(4):
                              hs = slice(h * 128, (h + 1) * 128)
                              mm(obk.t[:, hs], ATm.t[:, h, :], ivb[s].t[:, hs], True, False, [ATm, ivb[s]], [obk])
                              mm(obk.t[:, hs], qtT.t[:, h, :], Sbf.t[:, hs], False, True, [qtT, Sbf], [obk])
                          yield
                          st = h_stat()
                          act(t32[0].t[:], obk.t[:], AF.Square, [obk], [t32[0]])
                          rsum(st.t[:, 0:4], v3(t32[0].t[:], 4), [t32[0]], [st])
                          yield
                          r = rstd_from_ss(st, 128, 4)
                          tt("dve", v3(t32[1].t[:], 4), v3(obk.t[:], 4), bc_last(r, 128), ALU.mult, [obk, st], [t32[1]])
                          rel(obk)
                          tt("pool", tb16[1].t[:], t32[1].t[:], sgn[s].t[:], ALU.mult, [t32[1], sgn[s]], [tb16[1]])
                          yield
                          transpose4(tb16[1], yT[0], s * 128)
                          yield
                      for w_ in (wq_, wf_, wi_, wg_):
                          wrel(w_)

                  def gen_mem():
                      ws = wload(BLK_MQ)
                      for s in range(NSUB):
                          pb = proj_tok(ws, s)
                          tbm = m_tb[s % 2]
                          yield
                          hst = headnorm_a(pb, 4, m_tq, m_stat)
                          yield
                          headnorm_b(hst, pb, 4, gms.t[:], gms, tbm, m_tn)
                          rel(pb)
                          yield
                          yield
                          transpose4(tbm, mqT, 0)
                          yield
                          pts = []
                          for mt in range(2):
                              sbk = acq()
                              for h in range(4):
                                  mm(sbk.t[:, h * 128:(h + 1) * 128], KmT.t[:, h, mt * 128:(mt + 1) * 128], mqT.t[:, h, :],
                                     True, True, [KmT, mqT], [sbk])
                              pt = m_PT[mt]
                              act(pt.t[:], sbk.t[:], AF.Exp, [sbk], [pt])
                              rel(sbk)
                              pts.append(pt)
                              yield
                          yc = m_tb[(s + 1) % 2]
                          for hh in range(2):
                              obk = acq()
                              for h2 in range(2):
                                  h = hh * 2 + h2
                                  for mt in range(2):
                                      mm(obk.t[:, h2 * 256:h2 * 256 + 129], pts[mt].t[:, h * 128:(h + 1) * 128],
                                         Vm.t[:, mt, h, 0:129], mt == 0, mt == 1, [pts[mt], Vm], [obk])
                              ri = m_rinv[hh]
                              o3 = obk.t[:].rearrange("p (h d) -> p h d", h=2)
                              P.op("dve", lambda e, ri=ri, o3=o3: e.reciprocal(out=ri.t[:, 0:2].unsqueeze(2),
                                                                               in_=o3[:, :, 128:129]), [obk], [ri])
                              tt("dve", v3(yc.t[:, hh * 256:(hh + 1) * 256], 2), o3[:, :, 0:128], bc_last(ri.t[:, 0:2], 128),
                                 ALU.mult, [obk, ri], [yc])
                              rel(obk)
                              yield
                          transpose4(yc, yT[2], s * 128)
                          yield
                      wrel(ws)

                  if idx == 0:
                      issue_casts([5, 6])
                  n_fox = 16 + 2 * (16 * ti + 10)
                  run_gens([gen_fox(), gen_hgrn(), gen_mem()], [1, 1, 1])

                  ck('memattn')
                  if idx == 0:
                      issue_casts([7])
                  bslots = [wload(BLK_BR0 + i) for i in range(3)]
                  gslots = {}
                  for m in range(8):
                      sig = []
                      for r in range(3):
                          q = m * 3 + r
                          if q // 4 not in gslots:
                              gslots[q // 4] = wload(BLK_GATE0 + q // 4)
                          gsl, c0 = gslots[q // 4], (q % 4) * 128
                          pb = acq()
                          for kc in range(8):
                              mm(pb.t[:], gsl.t[:, kc * 512 + c0:kc * 512 + c0 + 128], hT.t[:, kc, :], kc == 0, kc == 7,
                                 [gsl, hT], [pb])
                          sg_ = t32[r]
                          act(sg_.t[:], pb.t[:], AF.Sigmoid, [pb], [sg_])
                          rel(pb)
                          sig.append(sg_)
                          if q % 4 == 3:
                              wrel(gsl)
                      acc, tmp = t32[3], t32[4]
                      for r in range(3):
                          pb = acq()
                          for kc in range(4):
                              mm(pb.t[:], bslots[r].t[:, kc * 1024 + m * 128:kc * 1024 + (m + 1) * 128], yT[r].t[:, kc, :],
                                 kc == 0, kc == 3, [bslots[r], yT[r]], [pb])
                          if r == 0:
                              tt("dve", acc.t[:], sig[0].t[:], pb.t[:], ALU.mult, [sig[0], pb], [acc])
                          else:
                              tt("dve", tmp.t[:], sig[r].t[:], pb.t[:], ALU.mult, [sig[r], pb], [tmp])
                              if r == 1:
                                  tt("pool", acc.t[:], acc.t[:], tmp.t[:], ALU.add, [acc, tmp], [acc])
                              else:
                                  tt("pool", mergedT.t[:, m, :], acc.t[:], tmp.t[:], ALU.add, [acc, tmp], [mergedT])
                          rel(pb)
                  for w_ in bslots:
                      wrel(w_)
                  ck('merge')
                  wo = [wload(BLK_WOUT0), wload(BLK_WOUT0 + 1)]
                  pend = []
                  for s in range(NSUB):
                      for cb in range(2):
                          pb = acq()
                          for kc in range(8):
                              mm(pb.t[:], mergedT.t[:, kc, s * 128:(s + 1) * 128], wo[cb].t[:, kc * 512:(kc + 1) * 512],
                                 kc == 0, kc == 7, [mergedT, wo[cb]], [pb])
                          xs = xr.t[:, s, cb * 512:(cb + 1) * 512]
                          tt("dve", xs, xs, pb.t[:], ALU.add, [xr, pb], [xr])
                          rel(pb)
                      if pend:
                          norm_p2(*pend.pop(0))
                      pend.append((norm_p1(xr.t[:, s, :], xr), GC_FFN, hT, s * 128))
                  norm_p2(*pend.pop(0))
                  for w_ in wo:
                      wrel(w_)
                  if idx + 1 < len(tiles):
                      issue_loads(idx + 1)
                  cw = cvp.t[:].rearrange("p (c k) -> p c k", k=4)
                  for u in range(11):
                      ws = wload(BLK_UP0 + u)
                      for jj in range(2):
                          j = 2 * u + jj
                          pa, pv = acq(), acq()
                          for kc in range(8):
                              c0 = kc * 512 + jj * 256
                              mm(pa.t[:], ws.t[:, c0:c0 + 128], hT.t[:, kc, :], kc == 0, kc == 7, [ws, hT], [pa])
                          for kc in range(8):
                              c0 = kc * 512 + jj * 256 + 128
                              mm(pv.t[:], ws.t[:, c0:c0 + 128], hT.t[:, kc, :], kc == 0, kc == 7, [ws, hT], [pv])
                          ab = abuf[j % 2]
                          cp("pool", ab.t[:, 0:2], aprev.t[:, j, :], [aprev], [ab])
                          cp("act", ab.t[:, 2:TT + 2], pa.t[:], [pa], [ab])
                          cp("pool", aprev.t[:, j, :], ab.t[:, TT:TT + 2], [ab], [aprev])
                          c0_, c1, c2, cg = t32[2 + j % 2], t32[0], t32[1], t32[4 + j % 2]
                          act(c0_.t[:], pa.t[:], AF.Identity, [pa, cvp], [c0_], scale=cw[:, j, 2:3], bias=cw[:, j, 3:4])
                          rel(pa)
                          stt("dve", c2.t[:], ab.t[:, 1:TT + 1], cw[:, j, 1:2], c0_.t[:], ALU.mult, ALU.add,
                              [ab, cvp, c0_], [c2])
                          stt("dve", c1.t[:], ab.t[:, 0:TT], cw[:, j, 0:1], c2.t[:], ALU.mult, ALU.add,
                              [ab, cvp, c2], [c1])
                          act(cg.t[:], c1.t[:], AF.Gelu, [c1], [cg])
                          tt("dve", ybuf.t[:, j, :], cg.t[:], pv.t[:], ALU.mult, [cg, pv], [ybg[j // 8]])
                          rel(pv)
                      wrel(ws)
                  def gen_down(xr=xr):
                      for cb in range(2):
                          obs = [acq() for _ in range(NSUB)]
                          for kb in range(3):
                              ws = wload(BLK_DN0 + cb * 3 + kb)
                              chs = list(range(kb * 8, min(kb * 8 + 8, NCH)))
                              for s in range(NSUB):
                                  for n, j in enumerate(chs):
                                      mm(obs[s].t[:], ybuf.t[:, j, s * 128:(s + 1) * 128], ws.t[:, n * 512:(n + 1) * 512],
                                         j == 0, j == NCH - 1, [ybg[kb], ws], [obs[s]])
                                  yield
                              wrel(ws)
                          for s in range(NSUB):
                              xs = xr.t[:, s, cb * 512:(cb + 1) * 512]
                              tt("dve", xs, xs, obs[s].t[:], ALU.add, [xr, obs[s]], [xr])
                              rel(obs[s])
                          yield

                  if idx + 1 < len(tiles):
                      run_gens([gen_down(), gen_prologue(idx + 1)], [1, 2])
                  else:
                      run_gens([gen_down()])
                  dma("pool", y_d[b, t0:t0 + TT, :].rearrange("(s p) d -> p s d", p=128), xr.t[:], ch_out[gt % 2],
                      [xr], [])

        except _Stop:
            pass

        fin = Buf("fin", None)
        for xr in xres:
            P.op("pool", lambda e: e.engine_nop(), [], [xr])

        P.finalize()
        sems = {}
        for e_ in ENGS:
            sems[e_] = es.enter_context(nc.semaphore(f"sem_{e_}"))
        for c in P.chans:
            c.sem = es.enter_context(nc.semaphore(f"ch_{c.name}"))
        with nc.Block() as block:
            @block.tensor
            def _(e):
                P.emit("pe", e, sems)

            @block.scalar
            def _(e):
                P.emit("act", e, sems)

            @block.vector
            def _(e):
                P.emit("dve", e, sems)

            @block.gpsimd
            def _(e):
                P.emit("pool", e, sems)

            @block.sync
            def _(e):
                P.emit("sp", e, sems)
    return nc


_NC_CACHE = {}


def _prep_shared(inp):
    f = lambda k: np.asarray(inp[k], np.float32)
    wb = _host_wblocks(f("w_in")[0], f("w_br_hgrn")[0], f("w_br_fox")[0], f("w_br_mem")[0], f("w_out")[0],
                       f("ffn_w_up")[0], f("ffn_w_down")[0], f("mem_kv_w")[0])
    par = np.zeros((1, NPAR), np.float32)
    gcol = np.zeros((128, 24), np.float32)
    gcol[:, GC_MIX:GC_MIX + 8] = f("norm_mix_g")[0].reshape(8, 128).T
    gcol[:, GC_MEM:GC_MEM + 8] = f("norm_mem_g")[0].reshape(8, 128).T
    gcol[:, GC_FFN:GC_FFN + 8] = f("norm_ffn_g")[0].reshape(8, 128).T
    par[0, PO_HNG:PO_HNG + 128] = f("hgrn_norm_g")[0]
    par[0, PO_FB:PO_FB + 8] = f("fox_f_bias")[0]
    par[0, PO_FQG:PO_FQG + 64] = f("fox_q_norm_g")[0]
    par[0, PO_FKG:PO_FKG + 64] = f("fox_k_norm_g")[0]
    par[0, PO_MQG:PO_MQG + 128] = f("mem_q_norm_g")[0]
    par[0, PO_MKG:PO_MKG + 128] = f("mem_k_norm_g")[0]
    cw = f("ffn_conv_w")[0].reshape(3, NCH, 128)
    cbias = f("ffn_conv_b")[0].reshape(NCH, 128)
    cvp = np.zeros((128, NCH, 4), np.float32)
    cvp[:, :, 0:3] = cw.transpose(2, 1, 0)
    cvp[:, :, 3] = cbias.T
    wff = f("w_in")[0][:, 3584:3592].reshape(8, 128, 8).transpose(1, 0, 2).reshape(128, 64)
    return dict(wblocks=wb, params=par, consts=_host_consts(), convp=np.ascontiguousarray(cvp.reshape(128, NCH * 4)),
                wff=np.ascontiguousarray(wff), gcol=gcol, lblog=np.ascontiguousarray(f("hgrn_lb_logits")))


def kernel(**inputs):
    x = np.asarray(inputs["x"], np.float32)
    mem = np.asarray(inputs["mem"], np.float32)
    B, T, _ = x.shape
    ncores = 8
    nseq = B // ncores
    key = (nseq, T)
    if key not in _NC_CACHE:
        _NC_CACHE[key] = build_nc(nseq, T)
    nc = _NC_CACHE[key]
    shared = _prep_shared(inputs)
    in_maps = []
    for c in range(ncores):
        m = dict(shared)
        m["x"] = np.ascontiguousarray(x[c * nseq:(c + 1) * nseq])
        m["mem"] = np.ascontiguousarray(mem[c * nseq:(c + 1) * nseq])
        in_maps.append(m)
    res = run_bass_kernel_spmd(nc, in_maps, core_ids=list(range(ncores)))
    out = np.concatenate([np.asarray(r["y"]) for r in res.results], axis=0)
    return out.astype(np.float32, copy=False)
```

```python
import contextlib
import numpy as np
import concourse.bass as bass
import concourse.mybir as mybir
from concourse.bass_utils import run_bass_kernel_spmd

F32 = mybir.dt.float32
BF16 = mybir.dt.bfloat16
AF = mybir.ActivationFunctionType
ALU = mybir.AluOpType
AX = mybir.AxisListType

D = 1024
DFF = 2816
NCH = DFF // 128
NMEM = 256
EPS = 1e-6
TT = 512
NSUB = 4
NSLOT = 6

BLK_FQ, BLK_FK, BLK_FV = 0, 1, 2
BLK_HQ, BLK_HF, BLK_HI, BLK_HG = 3, 4, 5, 6
BLK_MQ = 7
BLK_GATE0 = 8
BLK_BR0 = 14
BLK_WOUT0 = 17
BLK_UP0 = 19
BLK_DN0 = 30
BLK_MKV0 = 36
NBLK = 38

PO_HNG = 0
PO_FB = 128
PO_FQG, PO_FKG = 136, 200
PO_MQG, PO_MKG = 264, 392
NPAR = 520
GC_MIX, GC_MEM, GC_FFN = 0, 8, 16

CO_MMID, CO_UUP, CO_LINC, CO_HALF, CO_ONES, CO_MASK01, CO_MASKNEG, CO_IDENT = [i * 128 for i in range(8)]
CO_ONEHALF = 8 * 128
NCST = 8 * 128 + 2


class Buf:
    __slots__ = ("name", "t", "last_w", "readers", "aliases")

    def __init__(self, name, t):
        self.name, self.t = name, t
        self.last_w = None
        self.readers = []
        self.aliases = []


class Chan:
    def __init__(self, name, wait_total=False):
        self.name, self.wait_total = name, wait_total
        self.n = 0
        self.sem = None


class Op:
    __slots__ = ("eng", "fn", "deps", "needed", "val", "chan", "cn")

    def __init__(self, eng, fn, chan):
        self.eng, self.fn, self.chan = eng, fn, chan
        self.deps = []
        self.needed = False
        self.val = 0
        self.cn = 0


ENGS = ("pe", "act", "dve", "pool", "sp")


class Prog:
    def __init__(self):
        self.ops = {e: [] for e in ENGS}
        self.chans = []

    def chan(self, name, wait_total=False):
        c = Chan(name, wait_total)
        self.chans.append(c)
        return c

    def op(self, eng, fn, reads=(), writes=(), chan=None):
        o = Op(eng, fn, chan)
        if chan is not None:
            chan.n += 1
            o.cn = chan.n
        deps = {}

        def add(p, raw):
            if p is None:
                return
            if p.chan is None and o.chan is None and p.eng == eng:
                if (not raw) or eng == "pe":
                    return
            deps[id(p)] = p

        writes = list(writes) + [a for b in writes for a in b.aliases]
        for b in reads:
            add(b.last_w, True)
        for b in writes:
            add(b.last_w, False)
            for r in b.readers:
                add(r, False)
        for b in reads:
            b.readers.append(o)
        for b in writes:
            b.last_w = o
            b.readers = []
        o.deps = list(deps.values())
        for p in o.deps:
            p.needed = True
        self.ops[eng].append(o)
        return o

    def finalize(self):
        for e in ENGS:
            c = 0
            for o in self.ops[e]:
                if o.chan is None and o.needed:
                    c += 1
                    o.val = c

    def emit(self, engname, e, sems):
        seen = {}
        for o in self.ops[engname]:
            want = {}
            for p in o.deps:
                if p.chan is not None:
                    key = ("c", id(p.chan))
                    v = 16 * (p.chan.n if p.chan.wait_total else p.cn)
                    s = p.chan.sem
                else:
                    key = ("e", p.eng)
                    v = p.val
                    s = sems[p.eng]
                if v > want.get(key, (0, None))[0]:
                    want[key] = (v, s)
            for key, (v, s) in want.items():
                if seen.get(key, 0) >= v:
                    continue
                e.wait_ge(s, v)
                seen[key] = v
            ins = o.fn(e)
            if o.chan is not None:
                ins.then_inc(o.chan.sem, 16)
            elif o.needed:
                ins.then_inc(sems[engname], 1)


def _host_consts():
    s1 = np.arange(128)[:, None]
    s0 = np.arange(128)[None, :]
    c = np.zeros((128, NCST), np.float32)
    mmid = np.where((s1 > 63) & (s1 <= s0), 1.0, 0.0) - np.where((s1 <= 63) & (s1 > s0), 1.0, 0.0)
    c[:, CO_MMID:CO_MMID + 128] = mmid
    c[:, CO_UUP:CO_UUP + 128] = (s1 > s0)
    c[:, CO_LINC:CO_LINC + 128] = (s1 <= s0)
    c[:, CO_HALF:CO_HALF + 128] = (s1 <= 63) * np.ones((1, 128))
    c[:, CO_ONES:CO_ONES + 128] = 1.0
    c[:, CO_MASK01:CO_MASK01 + 128] = (s1 <= s0)
    c[:, CO_MASKNEG:CO_MASKNEG + 128] = np.where(s1 <= s0, 0.0, -30000.0)
    c[:, CO_IDENT:CO_IDENT + 128] = np.eye(128)
    c[:, CO_ONEHALF] = 1.0
    c[:, CO_ONEHALF + 1] = (np.arange(128) <= 63)
    return c


def _host_wblocks(w_in, w_br_hgrn, w_br_fox, w_br_mem, w_out, ffn_w_up, ffn_w_down, mem_kv_w):
    wb = np.zeros((NBLK, 128, 4096), np.float32)

    def k8(cols):
        return cols.reshape(8, 128, 512).transpose(1, 0, 2).reshape(128, 4096)

    col = {BLK_HQ: 0, BLK_HF: 512, BLK_HI: 1024, BLK_HG: 1536, BLK_FQ: 2048, BLK_FK: 2560,
           BLK_FV: 3072, BLK_MQ: 3592}
    for b, c0 in col.items():
        wb[b] = k8(w_in[:, c0:c0 + 512])
    g = w_in[:, 4104:4104 + 3072].reshape(1024, 3, 8, 128).transpose(0, 2, 1, 3).reshape(1024, 3072)
    for i in range(6):
        wb[BLK_GATE0 + i] = k8(g[:, i * 512:(i + 1) * 512])
    for i, w in enumerate((w_br_hgrn, w_br_fox, w_br_mem)):
        wb[BLK_BR0 + i] = w.reshape(4, 128, 1024).transpose(1, 0, 2).reshape(128, 4096)
    for i in range(2):
        wb[BLK_WOUT0 + i] = k8(w_out[:, i * 512:(i + 1) * 512])
    a, v = ffn_w_up[:, :DFF], ffn_w_up[:, DFF:]
    for u in range(11):
        cols = np.concatenate([a[:, (2 * u) * 128:(2 * u + 1) * 128], v[:, (2 * u) * 128:(2 * u + 1) * 128],
                               a[:, (2 * u + 1) * 128:(2 * u + 2) * 128], v[:, (2 * u + 1) * 128:(2 * u + 2) * 128]], axis=1)
        wb[BLK_UP0 + u] = k8(cols)
    wd = ffn_w_down.reshape(NCH, 128, 1024)
    for cb in range(2):
        for kb in range(3):
            chs = list(range(kb * 8, min(kb * 8 + 8, NCH)))
            blk = np.zeros((128, 8, 512), np.float32)
            for n, ch in enumerate(chs):
                blk[:, n, :] = wd[ch][:, cb * 512:(cb + 1) * 512]
            wb[BLK_DN0 + cb * 3 + kb] = blk.reshape(128, 4096)
    for i in range(2):
        wb[BLK_MKV0 + i] = k8(mem_kv_w[:, i * 512:(i + 1) * 512])
    return wb


def build_nc(nseq, T):
    ntile = T // TT
    nqb = T // 128
    nc = bass.Bass("TRN2", target_bir_lowering=False)
    x_d = nc.dram_tensor("x", [nseq, T, D], F32, kind="ExternalInput").ap()
    mem_d = nc.dram_tensor("mem", [nseq, NMEM, D], F32, kind="ExternalInput").ap()
    wb_d = nc.dram_tensor("wblocks", [NBLK, 128, 4096], F32, kind="ExternalInput").ap()
    par_d = nc.dram_tensor("params", [1, NPAR], F32, kind="ExternalInput").ap()
    cst_d = nc.dram_tensor("consts", [128, NCST], F32, kind="ExternalInput").ap()
    cvw_d = nc.dram_tensor("convp", [128, NCH * 4], F32, kind="ExternalInput").ap()
    wff_d = nc.dram_tensor("wff", [128, 64], F32, kind="ExternalInput").ap()
    gcol_d = nc.dram_tensor("gcol", [128, 24], F32, kind="ExternalInput").ap()
    lbl_d = nc.dram_tensor("lblog", [2, 512], F32, kind="ExternalInput").ap()
    y_d = nc.dram_tensor("y", [nseq, T, D], F32, kind="ExternalOutput").ap()
    wbf_d = nc.dram_tensor("wbf", [NBLK, 128, 4096], BF16, kind="Internal").ap()

    P = Prog()
    es = contextlib.ExitStack()

    def sb(name, shape, dt=F32):
        return Buf(name, es.enter_context(nc.sbuf_tensor(name, shape, dt)))

    def dram_buf(name):
        return Buf(name, None)

    with es:
        par = sb("par", [128, NPAR])
        cst = sb("cst", [128, NCST])
        cvp = sb("cvp", [128, NCH * 4])
        wff32 = sb("wff32", [128, 64])
        wffb = sb("wffb", [128, 8, 8], BF16)
        ident = sb("ident", [128, 128], BF16)
        maskneg = sb("maskneg", [128, 128], BF16)
        oml = sb("oml", [128, 512])
        gqs = sb("gqs", [128, 64])
        gms = sb("gms", [128, 128])
        gcol = sb("gcol_sb", [128, 24])
        xres = [sb(f"xres{i}", [128, NSUB, D]) for i in range(2)]
        hbf = [sb(f"hbf{i}", [128, D], BF16) for i in range(3)]
        hT = sb("hT", [128, 8, TT], BF16)
        U_t = es.enter_context(nc.sbuf_tensor("Ureg", [128, 12288], BF16))
        stat = [sb(f"stat{i}", [128, 24]) for i in range(4)]
        wslot = [sb(f"wslot{i}", [128, 4096], BF16) for i in range(NSLOT)]
        QT = Buf("QT", U_t[:, 0:2048].rearrange("p (a b) -> p a b", a=4))
        KT = sb("KT", [128, 4, T], BF16)
        Vaug = sb("Vaug", [128, nqb, 8, 66], BF16)
        negFc = sb("negFc", [128, nqb, 8])
        nCb = sb("nCb", [128, nqb, 8])
        ncarry = sb("ncarry", [128, 8])
        lfb = [sb(f"lfb{i}", [128, 8]) for i in range(2)]
        biasT = [sb(f"biasT{i}", [128, 8]) for i in range(4)]
        wtb = [sb(f"wtb{i}", [128, 8]) for i in range(4)]
        Vs = [sb(f"Vs{i}", [128, 8, 66], BF16) for i in range(3)]
        PT = [sb(f"PT{i}", [128, 512], BF16) for i in range(3)]
        t32 = [sb(f"t32_{i}", [128, 512]) for i in range(6)]
        tb16 = [sb(f"tb16_{i}", [128, 512], BF16) for i in range(4)]
        rinv = [sb(f"rinv{i}", [128, 8]) for i in range(2)]
        yT = [Buf(f"yT{i}", U_t[:, 2048 * (i + 1):2048 * (i + 2)].rearrange("p (a b) -> p a b", a=4))
              for i in range(3)]
        sq = [sb("sq0", [128, 512], BF16)] * NSUB
        kk = [sb("kk0", [128, 512])] * NSUB
        lgf = [sb("lgf0", [128, 512])] * NSUB
        ivb = [sb("ivb0", [128, 512], BF16)] * NSUB
        sgn = [sb("sgn0", [128, 512], BF16)] * NSUB
        Sst = sb("Sst", [128, 512])
        Sbf = sb("Sbf", [128, 512], BF16)
        dece = [sb(f"dece{i}", [128, 8]) for i in range(2)]
        qtT = sb("qtT", [128, 4, 128], BF16)
        ktT = sb("ktT", [128, 4, 128], BF16)
        ATm = sb("ATm", [128, 4, 128], BF16)
        KmT = sb("KmT", [128, 4, NMEM], BF16)
        Vm = sb("Vm", [128, 2, 4, 130], BF16)
        mqT = sb("mqT", [128, 4, 128], BF16)
        mergedT = Buf("mergedT", U_t[:, 8192:12288].rearrange("p (a b) -> p a b", a=8))
        abuf = [sb(f"abuf{i}", [128, TT + 2]) for i in range(2)]
        aprev = sb("aprev", [128, NCH, 2])
        ybuf = Buf("ybuf", U_t[:, 0:NCH * TT].rearrange("p (a b) -> p a b", a=NCH))
        QTs = [Buf(f"QT{i}", QT.t) for i in range(NSUB)]
        ybg = [Buf(f"ybuf_g{i}", ybuf.t) for i in range(3)]
        mix_bufs = [QT, mergedT] + yT + QTs
        for b_ in mix_bufs:
            b_.aliases = [ybuf] + ybg
        ybuf.aliases = list(mix_bufs)
        for b_ in ybg:
            b_.aliases = list(mix_bufs)

        banks = [Buf(f"bank{i}", es.enter_context(nc.psum_tensor(f"bank{i}", [128, 512], F32))) for i in range(8)]
        free_banks = list(range(8))

        def acq():
            assert free_banks, "out of PSUM banks"
            return banks[free_banks.pop(0)]

        def rel(b):
            free_banks.append(banks.index(b))

        def bf(bank):
            return bank.t[:].bitcast(BF16)

        wbf_blk = [dram_buf(f"wbf{i}") for i in range(NBLK)]
        cast_groups = [[BLK_MKV0], [BLK_MKV0 + 1], [BLK_FQ], [BLK_FK, BLK_FV],
                       [BLK_HQ, BLK_HF, BLK_HI, BLK_HG, BLK_MQ],
                       [BLK_BR0 + i for i in range(3)] + [BLK_GATE0 + i for i in range(6)],
                       [BLK_WOUT0, BLK_WOUT0 + 1] + [BLK_UP0 + i for i in range(11)],
                       [BLK_DN0 + i for i in range(6)]]
        ch_cast = [P.chan(f"cast{i}", True) for i in range(len(cast_groups))]
        ch_setup = P.chan("setup", True)
        ch_slot = [P.chan(f"slot{i}") for i in range(NSLOT)]
        ch_x = [P.chan(f"x{i}") for i in range(2)]
        ch_out = [P.chan(f"out{i}") for i in range(2)]
        ch_mem = P.chan("mem")

        def mm(out, lhsT, rhs, start, stop, reads, writes, skip=False):
            if skip:
                P.op("pe", lambda e: e.matmul(out, lhsT=lhsT, rhs=rhs, start=start, stop=stop, skip_group_check=True),
                     reads, writes)
            else:
                P.op("pe", lambda e: e.matmul(out, lhsT=lhsT, rhs=rhs, start=start, stop=stop), reads, writes)

        def tr(out, in_, reads, writes):
            P.op("pe", lambda e: e.transpose(out, in_, ident.t[:]), list(reads) + [ident], writes)

        def act(out, in_, func, reads, writes, scale=1.0, bias=0.0, accum=None):
            if accum is None:
                P.op("act", lambda e: e.activation(out=out, in_=in_, func=func, bias=bias, scale=scale), reads, writes)
            else:
                P.op("act", lambda e: e.activation(out=out, in_=in_, func=func, bias=bias, scale=scale,
                                                   accum_out=accum), reads, writes)

        def tt(eng, out, in0, in1, op, reads, writes):
            P.op(eng, lambda e: e.tensor_tensor(out=out, in0=in0, in1=in1, op=op), reads, writes)

        def cp(eng, out, in_, reads, writes):
            if eng == "act":
                P.op("act", lambda e: e.copy(out=out, in_=in_), reads, writes)
            else:
                P.op(eng, lambda e: e.tensor_copy(out=out, in_=in_), reads, writes)

        def ts(eng, out, in0, s1, s2, op0, op1, reads, writes):
            P.op(eng, lambda e: e.tensor_scalar(out=out, in0=in0, scalar1=s1, scalar2=s2, op0=op0, op1=op1),
                 reads, writes)

        def stt(eng, out, in0, scalar, in1, op0, op1, reads, writes):
            P.op(eng, lambda e: e.scalar_tensor_tensor(out=out, in0=in0, scalar=scalar, in1=in1, op0=op0, op1=op1),
                 reads, writes)

        def rsum(out, in_, reads, writes):
            P.op("dve", lambda e: e.reduce_sum(out=out, in_=in_, axis=AX.X), reads, writes)

        def dma(eng, out, in_, chan, reads, writes):
            P.op(eng, lambda e: e.dma_start(out=out, in_=in_), reads, writes, chan=chan)

        stat_i = [0]

        def next_stat():
            stat_i[0] = (stat_i[0] + 1) % 4
            return stat[stat_i[0]]

        def mkpool(name, n=3):
            bufs = [sb(f"{name}{i}", [128, 24]) for i in range(n)]
            idx = [0]

            def nxt():
                idx[0] = (idx[0] + 1) % n
                return bufs[idx[0]]
            return nxt

        f_stat, h_stat, m_stat = mkpool("fst"), mkpool("hst"), mkpool("mst")
        def uview(k):
            b_ = Buf(f"uv{k}", U_t[:, 8192 + k * 512:8192 + (k + 1) * 512])
            b_.aliases = [mergedT, ybuf] + ybg
            mergedT.aliases.append(b_)
            ybuf.aliases.append(b_)
            for g_ in ybg:
                g_.aliases.append(b_)
            return b_

        f_tq, f_tb, m_tq, m_tb, m_PT = uview(0), [uview(1), uview(2)], uview(3), [uview(4), uview(5)], [uview(6), uview(7)]
        f_tn = Buf("f_tn", abuf[0].t[:, 0:512]); f_tn.aliases = [abuf[0]]; abuf[0].aliases = [f_tn]
        m_tn = Buf("m_tn", abuf[1].t[:, 0:512]); m_tn.aliases = [abuf[1]]; abuf[1].aliases = [m_tn]
        m_rinv = [sb(f"mrinv{i}", [128, 8]) for i in range(2)]

        def rstd_from_ss(st, n, width):
            act(st.t[:, 8:8 + width], st.t[:, 0:width], AF.Ln, [st], [st], scale=1.0 / n, bias=EPS)
            act(st.t[:, 16:16 + width], st.t[:, 8:8 + width], AF.Exp, [st], [st], scale=-0.5)
            return st.t[:, 16:16 + width]

        def bc_mid(ap2, n):
            return ap2.unsqueeze(1).to_broadcast([128, n, ap2.shape[1]])

        def bc_last(ap2, n):
            return ap2.unsqueeze(2).to_broadcast([128, ap2.shape[1], n])

        def v3(ap2, a):
            return ap2.rearrange("p (a b) -> p a b", a=a)

        free_slots = list(range(NSLOT))

        def wload(blk):
            assert free_slots, "out of weight slots"
            k = free_slots.pop(0)
            s = wslot[k]
            dma("sp", s.t[:], wbf_d[blk], ch_slot[k], [wbf_blk[blk]], [s])
            return s

        def wrel(s):
            free_slots.append(wslot.index(s))

        hbf_i = [0]

        def norm_p1(src_ap, src_buf):
            st = next_stat()
            hbf_i[0] = (hbf_i[0] + 1) % 3
            hb = hbf[hbf_i[0]]
            act(hb.t[:], src_ap, AF.Square, [src_buf], [hb, st], accum=st.t[:, 0:1])
            r = rstd_from_ss(st, D, 1)
            ts("dve", hb.t[:], src_ap, r, None, ALU.mult, ALU.bypass, [src_buf, st], [hb])
            return hb

        def norm_transpose(src_ap, src_buf, gc0, dstT, col0):
            norm_p2(norm_p1(src_ap, src_buf), gc0, dstT, col0)

        def norm_p2(hb, gc0, dstT, col0):
            pb = acq()
            for kc in range(8):
                tr(bf(pb)[:, kc * 128:(kc + 1) * 128], hb.t[:, kc * 128:(kc + 1) * 128], [hb], [pb])
            tt("dve", dstT.t[:, :, col0:col0 + 128], v3(bf(pb), 8), bc_last(gcol.t[:, gc0:gc0 + 8], 128), ALU.mult,
               [pb, gcol], [dstT])
            rel(pb)

        def headnorm_to_bf(pb, nh, g_ap, gbuf, out_bf, tq=None, tn=None, statf=None):
            headnorm_b(headnorm_a(pb, nh, tq, statf), pb, nh, g_ap, gbuf, out_bf, tn)

        def headnorm_a(pb, nh, tq=None, statf=None):
            st = (statf or next_stat)()
            tq = tq or t32[0]
            act(tq.t[:], pb.t[:], AF.Square, [pb], [tq])
            rsum(st.t[:, 0:nh], v3(tq.t[:], nh), [tq], [st])
            return st

        def headnorm_b(st, pb, nh, g_ap, gbuf, out_bf, tn=None):
            hd = 512 // nh
            r = rstd_from_ss(st, hd, nh)
            tn = tn or t32[1]
            tt("dve", v3(tn.t[:], nh), v3(pb.t[:], nh), bc_last(r, hd), ALU.mult, [pb, st], [tn])
            tt("pool", v3(out_bf.t[:], nh), v3(tn.t[:], nh), bc_mid(g_ap, nh), ALU.mult, [tn, gbuf], [out_bf])

        def transpose4(src_bf, dstT, col0):
            pb = acq()
            for c in range(4):
                tr(bf(pb)[:, c * 128:(c + 1) * 128], src_bf.t[:, c * 128:(c + 1) * 128], [src_bf], [pb])
            cp("dve", dstT.t[:, :, col0:col0 + 128], v3(bf(pb)[:, 0:512], 4), [pb], [dstT])
            rel(pb)

        def proj_tok(ws, s, nk=8):
            pb = acq()
            for kc in range(nk):
                mm(pb.t[:], hT.t[:, kc, s * 128:(s + 1) * 128], ws.t[:, kc * 512:(kc + 1) * 512],
                   kc == 0, kc == nk - 1, [hT, ws], [pb])
            return pb

        def issue_casts(grps):
            for grp in grps:
                for b in cast_groups[grp]:
                    dma("pool", wbf_d[b], wb_d[b], ch_cast[grp], [], [wbf_blk[b]])

        issue_casts(range(0, 5))
        dma("sp", par.t[:], par_d.partition_broadcast(128), ch_setup, [], [par])
        dma("sp", gcol.t[:], gcol_d, ch_setup, [], [gcol])
        dma("sp", t32[0].t[:], lbl_d[0:1, :].partition_broadcast(128), ch_setup, [], [t32[0]])
        dma("sp", t32[1].t[:], lbl_d[1:2, :].partition_broadcast(128), ch_setup, [], [t32[1]])
        dma("sp", cst.t[:], cst_d, ch_setup, [], [cst])
        dma("sp", cvp.t[:], cvw_d, ch_setup, [], [cvp])
        dma("sp", wff32.t[:], wff_d, ch_setup, [], [wff32])
        cp("dve", ident.t[:], cst.t[:, CO_IDENT:CO_IDENT + 128], [cst], [ident])
        cp("dve", maskneg.t[:], cst.t[:, CO_MASKNEG:CO_MASKNEG + 128], [cst], [maskneg])
        cp("dve", wffb.t[:].rearrange("p a b -> p (a b)"), wff32.t[:], [wff32], [wffb])
        tt("dve", oml.t[:], t32[1].t[:], t32[0].t[:], ALU.subtract, [t32[0], t32[1]], [oml])
        act(oml.t[:], oml.t[:], AF.Sigmoid, [oml], [oml])
        ts("dve", gqs.t[:], par.t[:, PO_FQG:PO_FQG + 64], 0.125, None, ALU.mult, ALU.bypass, [par], [gqs])
        ts("dve", gms.t[:], par.t[:, PO_MQG:PO_MQG + 128], float(128 ** -0.5), None, ALU.mult, ALU.bypass, [par], [gms])
        P.op("pool", lambda e: e.memset(Vaug.t[:], 1.0), [], [Vaug])
        P.op("pool", lambda e: e.memset(Vm.t[:], 1.0), [], [Vm])

        hng = par.t[:, PO_HNG:PO_HNG + 128]
        fkg = par.t[:, PO_FKG:PO_FKG + 64]
        mkg = par.t[:, PO_MKG:PO_MKG + 128]
        fbias = par.t[:, PO_FB:PO_FB + 8]
        mask01 = cst.t[:, CO_MASK01:CO_MASK01 + 128]

        import os as _os
        _stop = _os.environ.get("KSTOP", "")

        class _Stop(Exception):
            pass

        def ck(name):
            if _stop == name:
                raise _Stop()

        gt = 0
        try:
          ck("setup")
          tiles = [(b_, ti_) for b_ in range(nseq) for ti_ in range(ntile)]

          def issue_loads(idx):
              b, ti = tiles[idx]
              xr = xres[idx % 2]
              if ti == 0:
                  dma("sp", xr.t[:, 0:2, :], mem_d[b].rearrange("(s p) d -> p s d", p=128), ch_mem, [], [xr])
              else:
                  dma("sp", xr.t[:], x_d[b, ti * TT:(ti + 1) * TT, :].rearrange("(s p) d -> p s d", p=128),
                      ch_x[idx % 2], [], [xr])

          def gen_prologue(idx):
              b, ti = tiles[idx]
              xr = xres[idx % 2]
              if ti == 0:
                  memx, memhT = xr, hT
                  hbs = []
                  for s in range(2):
                      hbs.append(norm_p1(memx.t[:, s, :], memx))
                      yield
                  dma("sp", xr.t[:], x_d[b, 0:TT, :].rearrange("(s p) d -> p s d", p=128), ch_x[idx % 2], [], [xr])
                  for s in range(2):
                      norm_p2(hbs[s], GC_MEM, memhT, s * 128)
                      yield
                  wk = wload(BLK_MKV0)
                  for s in range(2):
                      pb = acq()
                      for kc in range(8):
                          mm(pb.t[:], memhT.t[:, kc, s * 128:(s + 1) * 128], wk.t[:, kc * 512:(kc + 1) * 512],
                             kc == 0, kc == 7, [memhT, wk], [pb])
                      yield
                      headnorm_to_bf(pb, 4, mkg, par, tb16[0])
                      rel(pb)
                      yield
                      transpose4(tb16[0], KmT, s * 128)
                      yield
                  wrel(wk)
                  wv = wload(BLK_MKV0 + 1)
                  for s in range(2):
                      pb = acq()
                      for kc in range(8):
                          mm(pb.t[:], memhT.t[:, kc, s * 128:(s + 1) * 128], wv.t[:, kc * 512:(kc + 1) * 512],
                             kc == 0, kc == 7, [memhT, wv], [pb])
                      cp("act", Vm.t[:, s, :, 0:128], v3(pb.t[:], 4), [pb], [Vm])
                      rel(pb)
                      yield
                  wrel(wv)
                  P.op("pool", lambda e: e.memset(Sst.t[:], 0.0), [], [Sst])
                  P.op("pool", lambda e: e.memset(aprev.t[:], 0.0), [], [aprev])
                  P.op("pool", lambda e: e.memset(ncarry.t[:], 0.0), [], [ncarry])
              hbs = []
              for s in range(NSUB):
                  if len(hbs) == 2:
                      norm_p2(hbs.pop(0), GC_MIX, hT, (s - 2) * 128)
                      yield
                  hbs.append(norm_p1(xr.t[:, s, :], xr))
                  yield
              for s in (NSUB - 2, NSUB - 1):
                  norm_p2(hbs.pop(0), GC_MIX, hT, s * 128)
                  yield
              t0 = ti * TT
              ws = wload(BLK_FK)
              wsv = wload(BLK_FV)
              pbs = {}
              for step in range(NSUB + 3):
                  if step < NSUB:
                      pbs[step] = proj_tok(ws, step)
                      yield
                  if 0 <= step - 1 < NSUB:
                      s1 = step - 1
                      hst = headnorm_a(pbs[s1], 8, None, f_stat)
                      yield
                      headnorm_b(hst, pbs[s1], 8, fkg, par, tb16[s1 % 4], None)
                      rel(pbs.pop(s1))
                      yield
                  if 0 <= step - 2 < NSUB:
                      s2 = step - 2
                      pb = proj_tok(wsv, s2)
                      cp("act", Vaug.t[:, ti * NSUB + s2, :, 0:64], v3(pb.t[:], 8), [pb], [Vaug])
                      rel(pb)
                      yield
                  if 0 <= step - 3 < NSUB:
                      s3 = step - 3
                      transpose4(tb16[s3 % 4], KT, t0 + s3 * 128)
                      yield
              wrel(ws)
              wrel(wsv)
              for s in range(NSUB):
                  qi = ti * NSUB + s
                  pb = acq()
                  for kc in range(8):
                      mm(pb.t[:, 0:8], hT.t[:, kc, s * 128:(s + 1) * 128], wffb.t[:, kc, :], kc == 0, kc == 7,
                         [hT, wffb], [pb])
                  lf = lfb[s % 2]
                  tt("dve", lf.t[:], pb.t[:, 0:8], fbias, ALU.add, [pb, par], [lf])
                  act(lf.t[:], lf.t[:], AF.Exp, [lf], [lf], scale=-1.0)
                  act(lf.t[:], lf.t[:], AF.Ln, [lf], [lf], bias=1.0)
                  yield
                  mm(pb.t[:, 16:24], cst.t[:, CO_LINC:CO_LINC + 128], lf.t[:], True, True, [cst, lf], [pb])
                  mm(pb.t[:, 24:32], cst.t[:, CO_HALF:CO_HALF + 128], lf.t[:], True, True, [cst, lf], [pb])
                  mm(pb.t[:, 32:40], cst.t[:, CO_ONES:CO_ONES + 128], lf.t[:], True, True, [cst, lf], [pb])
                  tt("dve", negFc.t[:, qi, :], pb.t[:, 16:24], ncarry.t[:], ALU.add, [pb, ncarry], [negFc])
                  tt("dve", nCb.t[:, qi, :], pb.t[:, 24:32], ncarry.t[:], ALU.add, [pb, ncarry], [nCb])
                  tt("dve", ncarry.t[:], pb.t[:, 32:40], ncarry.t[:], ALU.add, [pb, ncarry], [ncarry])
                  rel(pb)
                  yield

          def run_gens(gens, weights=None):
              gens = list(gens)
              weights = list(weights or [1] * len(gens))
              while gens:
                  for g_ in list(gens):
                      w_ = weights[gens.index(g_)]
                      for _ in range(w_):
                          try:
                              next(g_)
                          except StopIteration:
                              k_ = gens.index(g_)
                              gens.pop(k_)
                              weights.pop(k_)
                              break

          issue_loads(0)
          run_gens([gen_prologue(0)])
          for idx, (b, ti) in enumerate(tiles):
              if True:
                  gt = idx
                  xr = xres[idx % 2]
                  t0 = ti * TT
                  ck('A')
                  def gen_fox(ti=ti, t0=t0):
                      ws = wload(BLK_FQ)
                      qstage = {}

                      def q_stages(s):
                          st_ = {}

                          def a():
                              st_["pb"] = proj_tok(ws, s)

                          def b():
                              st_["st"] = headnorm_a(st_["pb"], 8, f_tq, f_stat)

                          def b2():
                              headnorm_b(st_["st"], st_["pb"], 8, gqs.t[:], gqs, f_tb[s % 2], f_tn)
                              rel(st_["pb"])

                          def c():
                              transpose4(f_tb[s % 2], QTs[s], s * 128)
                          return [a, b, b2, c]

                      for f_ in q_stages(0):
                          f_()
                          yield
                      pend_q = []
                      steps = []
                      for s in range(NSUB):
                          for j in range(ti * NSUB + s + 1):
                              for hh in range(2):
                                  steps.append((s, j, hh))
                      st_sbk, st_pt, st_vs = {}, {}, {}
                      obs_ = {}

                      def att_prep(k):
                          s, j, hh = steps[k]
                          qi = ti * NSUB + s
                          if hh == 0:
                              n = k // 2
                              bt, wt_, vs = biasT[n % 4], wtb[n % 4], Vs[n % 3]
                              tt("dve", bt.t[:], negFc.t[:, j, :], nCb.t[:, qi, :], ALU.subtract, [negFc, nCb], [bt])
                              act(wt_.t[:], bt.t[:], AF.Exp, [bt], [wt_])
                              tt("pool" if n % 2 else "dve", vs.t[:, :, 0:65], Vaug.t[:, j, :, 0:65], bc_last(wt_.t[:], 65),
                                 ALU.mult, [Vaug, wt_], [vs])
                              st_vs[(s, j)] = vs

                      def att_st(k):
                          s, j, hh = steps[k]
                          qi = ti * NSUB + s
                          sbk = acq()
                          for h4 in range(4):
                              r0 = hh * 64
                              mm(sbk.t[:, h4 * 128:(h4 + 1) * 128], KT.t[r0:r0 + 64, h4, j * 128:(j + 1) * 128],
                                 QT.t[r0:r0 + 64, h4, s * 128:(s + 1) * 128], True, j != qi, [KT, QTs[s]], [sbk])
                              if j == qi:
                                  mm(sbk.t[:, h4 * 128:(h4 + 1) * 128], ident.t[:], maskneg.t[:], False, True,
                                     [ident, maskneg], [sbk])
                          st_sbk[k] = sbk

                      def att_exp(k):
                          sbk = st_sbk.pop(k)
                          pt = PT[k % 3]
                          act(pt.t[:], sbk.t[:], AF.Exp, [sbk], [pt])
                          rel(sbk)
                          st_pt[k] = pt

                      def att_pv(k):
                          s, j, hh = steps[k]
                          qi = ti * NSUB + s
                          pt = st_pt.pop(k)
                          vs = st_vs[(s, j)]
                          if (s, hh) not in obs_:
                              obs_[(s, hh)] = acq()
                          ob_ = obs_[(s, hh)]
                          for h4 in range(4):
                              h = h4 * 2 + hh
                              mm(ob_.t[:, h4 * 128:h4 * 128 + 65], pt.t[:, h4 * 128:(h4 + 1) * 128],
                                 vs.t[:, h, 0:65], j == 0 and h4 == 0, False, [pt, vs], [ob_], skip=True)
                          if j == qi and hh == 1:
                              yb = f_tb[s % 2]
                              for h2 in range(2):
                                  ob2 = obs_.pop((s, h2))
                                  ri = rinv[h2]
                                  o3 = ob2.t[:].rearrange("p (h d) -> p h d", h=4)
                                  P.op("dve", lambda e, ri=ri, o3=o3: e.reciprocal(out=ri.t[:, 0:4].unsqueeze(2),
                                                                                   in_=o3[:, :, 64:65]), [ob2], [ri])
                                  yb4 = yb.t[:].rearrange("p (a b d) -> p a b d", a=4, b=2)[:, :, h2, :]
                                  tt("dve", yb4, o3[:, :, 0:64], bc_last(ri.t[:, 0:4], 64), ALU.mult, [ob2, ri], [yb])
                                  rel(ob2)
                              transpose4(yb, yT[1], s * 128)

                      LAG = 2
                      q_done = {0}

                      def need_q(k):
                          s_ = steps[k][0]
                          if s_ not in q_done:
                              while pend_q:
                                  pend_q.pop(0)()
                              q_done.add(s_)

                      for k0 in range(min(LAG, len(steps))):
                          need_q(k0)
                          att_prep(k0)
                          att_st(k0)
                      TRAIL = 2
                      for k in range(len(steps) + TRAIL):
                          if k < len(steps):
                              s_k = steps[k][0]
                              if s_k + 1 < NSUB and (s_k + 1) not in qstage:
                                  qstage[s_k + 1] = True
                                  pend_q.extend(q_stages(s_k + 1))
                          if k + LAG < len(steps):
                              need_q(k + LAG)
                              att_prep(k + LAG)
                              att_st(k + LAG)
                          if k < len(steps):
                              att_exp(k)
                          if k >= TRAIL:
                              att_pv(k - TRAIL)
                          if pend_q and k % 2 == 1:
                              pend_q.pop(0)()
                          yield
                      wrel(ws)

                  def gen_hgrn():
                      wq_, wf_, wi_, wg_ = wload(BLK_HQ), wload(BLK_HF), wload(BLK_HI), wload(BLK_HG)
                      for s in range(NSUB):
                          def sigm(pb_, sc, dst):
                              act(dst.t[:], pb_.t[:], AF.Exp, [pb_], [dst], scale=sc)
                              yield
                              act(dst.t[:], dst.t[:], AF.Ln, [dst], [dst], bias=1.0)
                              yield
                              act(dst.t[:], dst.t[:], AF.Exp, [dst], [dst], scale=-1.0)

                          pb = proj_tok(wq_, s)
                          yield
                          yield from sigm(pb, -1.0, t32[2])
                          tt("dve", sq[s].t[:], pb.t[:], t32[2].t[:], ALU.mult, [pb, t32[2]], [sq[s]])
                          rel(pb)
                          yield
                          pb = proj_tok(wf_, s)
                          yield
                          yield from sigm(pb, 1.0, t32[2])
                          rel(pb)
                          tt("dve", kk[s].t[:], t32[2].t[:], oml.t[:], ALU.mult, [t32[2], oml], [kk[s]])
                          act(lgf[s].t[:], kk[s].t[:], AF.Ln, [kk[s]], [lgf[s]], scale=-1.0, bias=1.0)
                          yield
                          pb = proj_tok(wi_, s)
                          cp("dve", ivb[s].t[:], pb.t[:], [pb], [ivb[s]])
                          rel(pb)
                          yield
                          pb = proj_tok(wg_, s)
                          yield
                          yield from sigm(pb, -1.0, t32[2])
                          tt("dve", t32[2].t[:], pb.t[:], t32[2].t[:], ALU.mult, [pb, t32[2]], [t32[2]])
                          rel(pb)
                          tt("pool", v3(sgn[s].t[:], 4), v3(t32[2].t[:], 4), bc_mid(hng, 4), ALU.mult, [t32[2], par], [sgn[s]])
                          yield
                          e1, e2, e3 = t32[3], t32[4], t32[5]
                          gm = acq()
                          mm(gm.t[:], cst.t[:, CO_MMID:CO_MMID + 128], lgf[s].t[:], True, True, [cst, lgf[s]], [gm])
                          yield
                          act(e1.t[:], gm.t[:], AF.Exp, [gm], [e1])
                          yield
                          act(e2.t[:], gm.t[:], AF.Exp, [gm], [e2], scale=-1.0)
                          rel(gm)
                          yield
                          gl = acq()
                          mm(gl.t[:], cst.t[:, CO_UUP:CO_UUP + 128], lgf[s].t[:], True, True, [cst, lgf[s]], [gl])
                          yield
                          act(e3.t[:], gl.t[:], AF.Exp, [gl], [e3])
                          rel(gl)
                          yield
                          gtb = acq()
                          for h in range(4):
                              mm(gtb.t[:, 2 * h:2 * h + 2], lgf[s].t[:, h * 128:(h + 1) * 128],
                                 cst.t[:, CO_ONEHALF:CO_ONEHALF + 2], True, True, [cst, lgf[s]], [gtb])
                          de = dece[s % 2]
                          act(de.t[:], gtb.t[:, 0:8], AF.Exp, [gtb], [de])
                          rel(gtb)
                          yield
                          qt_, kt_, kp_ = tb16[2], tb16[3], tb16[0]
                          tt("dve", qt_.t[:], sq[s].t[:], e1.t[:], ALU.mult, [sq[s], e1], [qt_])
                          tt("dve", kt_.t[:], kk[s].t[:], e2.t[:], ALU.mult, [kk[s], e2], [kt_])
                          tt("pool", kp_.t[:], kk[s].t[:], e3.t[:], ALU.mult, [kk[s], e3], [kp_])
                          yield
                          transpose4(qt_, qtT, 0)
                          transpose4(kt_, ktT, 0)
                          de3 = de.t[:].rearrange("p (h t) -> p h t", h=4)
                          tt("dve", v3(Sbf.t[:], 4), v3(Sst.t[:], 4), de3[:, :, 1:2].to_broadcast([128, 4, 128]), ALU.mult,
                             [Sst, de], [Sbf])
                          yield
                          atb = acq()
                          for h in range(4):
                              mm(atb.t[:, h * 128:(h + 1) * 128], ktT.t[:, h, :], qtT.t[:, h, :], True, True, [ktT, qtT], [atb])
                          tt("dve", ATm.t[:], v3(atb.t[:], 4), bc_mid(mask01, 4), ALU.mult, [atb, cst], [ATm])
                          rel(atb)
                          yield
                          sup = acq()
                          for h in range(4):
                              hs = slice(h * 128, (h + 1) * 128)
                              mm(sup.t[:, hs], kp_.t[:, hs], ivb[s].t[:, hs], True, True, [kp_, ivb[s]], [sup])
                          tt("dve", v3(Sst.t[:], 4), v3(Sst.t[:], 4), de3[:, :, 0:1].to_broadcast([128, 4, 128]), ALU.mult,
                             [Sst, de], [Sst])
                          tt("dve", Sst.t[:], Sst.t[:], sup.t[:], ALU.add, [Sst, sup], [Sst])
                          rel(sup)
                          yield
                          obk = acq()
                          for h in range(4):
                              hs = slice(h * 128, (h + 1) * 128)
                              mm(obk.t[:, hs], ATm.t[:, h, :], ivb[s].t[:, hs], True, False, [ATm, ivb[s]], [obk])
                              mm(obk.t[:, hs], qtT.t[:, h, :], Sbf.t[:, hs], False, True, [qtT, Sbf], [obk])
                          yield
                          st = h_stat()
                          act(t32[0].t[:], obk.t[:], AF.Square, [obk], [t32[0]])
                          rsum(st.t[:, 0:4], v3(t32[0].t[:], 4), [t32[0]], [st])
                          yield
                          r = rstd_from_ss(st, 128, 4)
                          tt("dve", v3(t32[1].t[:], 4), v3(obk.t[:], 4), bc_last(r, 128), ALU.mult, [obk, st], [t32[1]])
                          rel(obk)
                          tt("pool", tb16[1].t[:], t32[1].t[:], sgn[s].t[:], ALU.mult, [t32[1], sgn[s]], [tb16[1]])
                          yield
                          transpose4(tb16[1], yT[0], s * 128)
                          yield
                      for w_ in (wq_, wf_, wi_, wg_):
                          wrel(w_)

                  def gen_mem():
                      ws = wload(BLK_MQ)
                      for s in range(NSUB):
                          pb = proj_tok(ws, s)
                          tbm = m_tb[s % 2]
                          yield
                          hst = headnorm_a(pb, 4, m_tq, m_stat)
                          yield
                          headnorm_b(hst, pb, 4, gms.t[:], gms, tbm, m_tn)
                          rel(pb)
                          yield
                          yield
                          transpose4(tbm, mqT, 0)
                          yield
                          pts = []
                          for mt in range(2):
                              sbk = acq()
                              for h in range(4):
                                  mm(sbk.t[:, h * 128:(h + 1) * 128], KmT.t[:, h, mt * 128:(mt + 1) * 128], mqT.t[:, h, :],
                                     True, True, [KmT, mqT], [sbk])
                              pt = m_PT[mt]
                              act(pt.t[:], sbk.t[:], AF.Exp, [sbk], [pt])
                              rel(sbk)
                              pts.append(pt)
                              yield
                          yc = m_tb[(s + 1) % 2]
                          for hh in range(2):
                              obk = acq()
                              for h2 in range(2):
                                  h = hh * 2 + h2
                                  for mt in range(2):
                                      mm(obk.t[:, h2 * 256:h2 * 256 + 129], pts[mt].t[:, h * 128:(h + 1) * 128],
                                         Vm.t[:, mt, h, 0:129], mt == 0, mt == 1, [pts[mt], Vm], [obk])
                              ri = m_rinv[hh]
                              o3 = obk.t[:].rearrange("p (h d) -> p h d", h=2)
                              P.op("dve", lambda e, ri=ri, o3=o3: e.reciprocal(out=ri.t[:, 0:2].unsqueeze(2),
                                                                               in_=o3[:, :, 128:129]), [obk], [ri])
                              tt("dve", v3(yc.t[:, hh * 256:(hh + 1) * 256], 2), o3[:, :, 0:128], bc_last(ri.t[:, 0:2], 128),
                                 ALU.mult, [obk, ri], [yc])
                              rel(obk)
                              yield
                          transpose4(yc, yT[2], s * 128)
                          yield
                      wrel(ws)

                  if idx == 0:
                      issue_casts([5, 6])
                  n_fox = 16 + 2 * (16 * ti + 10)
                  run_gens([gen_fox(), gen_hgrn(), gen_mem()], [1, 1, 1])

                  ck('memattn')
                  if idx == 0:
                      issue_casts([7])
                  bslots = [wload(BLK_BR0 + i) for i in range(3)]
                  gslots = {}
                  for m in range(8):
                      sig = []
                      for r in range(3):
                          q = m * 3 + r
                          if q // 4 not in gslots:
                              gslots[q // 4] = wload(BLK_GATE0 + q // 4)
                          gsl, c0 = gslots[q // 4], (q % 4) * 128
                          pb = acq()
                          for kc in range(8):
                              mm(pb.t[:], gsl.t[:, kc * 512 + c0:kc * 512 + c0 + 128], hT.t[:, kc, :], kc == 0, kc == 7,
                                 [gsl, hT], [pb])
                          sg_ = t32[r]
                          act(sg_.t[:], pb.t[:], AF.Sigmoid, [pb], [sg_])
                          rel(pb)
                          sig.append(sg_)
                          if q % 4 == 3:
                              wrel(gsl)
                      acc, tmp = t32[3], t32[4]
                      for r in range(3):
                          pb = acq()
                          for kc in range(4):
                              mm(pb.t[:], bslots[r].t[:, kc * 1024 + m * 128:kc * 1024 + (m + 1) * 128], yT[r].t[:, kc, :],
                                 kc == 0, kc == 3, [bslots[r], yT[r]], [pb])
                          if r == 0:
                              tt("dve", acc.t[:], sig[0].t[:], pb.t[:], ALU.mult, [sig[0], pb], [acc])
                          else:
                              tt("dve", tmp.t[:], sig[r].t[:], pb.t[:], ALU.mult, [sig[r], pb], [tmp])
                              if r == 1:
                                  tt("pool", acc.t[:], acc.t[:], tmp.t[:], ALU.add, [acc, tmp], [acc])
                              else:
                                  tt("pool", mergedT.t[:, m, :], acc.t[:], tmp.t[:], ALU.add, [acc, tmp], [mergedT])
                          rel(pb)
                  for w_ in bslots:
                      wrel(w_)
                  ck('merge')
                  wo = [wload(BLK_WOUT0), wload(BLK_WOUT0 + 1)]
                  pend = []
                  for s in range(NSUB):
                      for cb in range(2):
                          pb = acq()
                          for kc in range(8):
                              mm(pb.t[:], mergedT.t[:, kc, s * 128:(s + 1) * 128], wo[cb].t[:, kc * 512:(kc + 1) * 512],
                                 kc == 0, kc == 7, [mergedT, wo[cb]], [pb])
                          xs = xr.t[:, s, cb * 512:(cb + 1) * 512]
                          tt("dve", xs, xs, pb.t[:], ALU.add, [xr, pb], [xr])
                          rel(pb)
                      if pend:
                          norm_p2(*pend.pop(0))
                      pend.append((norm_p1(xr.t[:, s, :], xr), GC_FFN, hT, s * 128))
                  norm_p2(*pend.pop(0))
                  for w_ in wo:
                      wrel(w_)
                  if idx + 1 < len(tiles):
                      issue_loads(idx + 1)
                  cw = cvp.t[:].rearrange("p (c k) -> p c k", k=4)
                  for u in range(11):
                      ws = wload(BLK_UP0 + u)
                      for jj in range(2):
                          j = 2 * u + jj
                          pa, pv = acq(), acq()
                          for kc in range(8):
                              c0 = kc * 512 + jj * 256
                              mm(pa.t[:], ws.t[:, c0:c0 + 128], hT.t[:, kc, :], kc == 0, kc == 7, [ws, hT], [pa])
                          for kc in range(8):
                              c0 = kc * 512 + jj * 256 + 128
                              mm(pv.t[:], ws.t[:, c0:c0 + 128], hT.t[:, kc, :], kc == 0, kc == 7, [ws, hT], [pv])
                          ab = abuf[j % 2]
                          cp("pool", ab.t[:, 0:2], aprev.t[:, j, :], [aprev], [ab])
                          cp("act", ab.t[:, 2:TT + 2], pa.t[:], [pa], [ab])
                          cp("pool", aprev.t[:, j, :], ab.t[:, TT:TT + 2], [ab], [aprev])
                          c0_, c1, c2, cg = t32[2 + j % 2], t32[0], t32[1], t32[4 + j % 2]
                          act(c0_.t[:], pa.t[:], AF.Identity, [pa, cvp], [c0_], scale=cw[:, j, 2:3], bias=cw[:, j, 3:4])
                          rel(pa)
                          stt("dve", c2.t[:], ab.t[:, 1:TT + 1], cw[:, j, 1:2], c0_.t[:], ALU.mult, ALU.add,
                              [ab, cvp, c0_], [c2])
                          stt("dve", c1.t[:], ab.t[:, 0:TT], cw[:, j, 0:1], c2.t[:], ALU.mult, ALU.add,
                              [ab, cvp, c2], [c1])
                          act(cg.t[:], c1.t[:], AF.Gelu, [c1], [cg])
                          tt("dve", ybuf.t[:, j, :], cg.t[:], pv.t[:], ALU.mult, [cg, pv], [ybg[j // 8]])
                          rel(pv)
                      wrel(ws)
                  def gen_down(xr=xr):
                      for cb in range(2):
                          obs = [acq() for _ in range(NSUB)]
                          for kb in range(3):
                              ws = wload(BLK_DN0 + cb * 3 + kb)
                              chs = list(range(kb * 8, min(kb * 8 + 8, NCH)))
                              for s in range(NSUB):
                                  for n, j in enumerate(chs):
                                      mm(obs[s].t[:], ybuf.t[:, j, s * 128:(s + 1) * 128], ws.t[:, n * 512:(n + 1) * 512],
                                         j == 0, j == NCH - 1, [ybg[kb], ws], [obs[s]])
                                  yield
                              wrel(ws)
                          for s in range(NSUB):
                              xs = xr.t[:, s, cb * 512:(cb + 1) * 512]
                              tt("dve", xs, xs, obs[s].t[:], ALU.add, [xr, obs[s]], [xr])
                              rel(obs[s])
                          yield

                  if idx + 1 < len(tiles):
                      run_gens([gen_down(), gen_prologue(idx + 1)], [2, 1])
                  else:
                      run_gens([gen_down()])
                  dma("pool", y_d[b, t0:t0 + TT, :].rearrange("(s p) d -> p s d", p=128), xr.t[:], ch_out[gt % 2],
                      [xr], [])

        except _Stop:
            pass

        fin = Buf("fin", None)
        for xr in xres:
            P.op("pool", lambda e: e.engine_nop(), [], [xr])

        P.finalize()
        sems = {}
        for e_ in ENGS:
            sems[e_] = es.enter_context(nc.semaphore(f"sem_{e_}"))
        for c in P.chans:
            c.sem = es.enter_context(nc.semaphore(f"ch_{c.name}"))
        with nc.Block() as block:
            @block.tensor
            def _(e):
                P.emit("pe", e, sems)

            @block.scalar
            def _(e):
                P.emit("act", e, sems)

            @block.vector
            def _(e):
                P.emit("dve", e, sems)

            @block.gpsimd
            def _(e):
                P.emit("pool", e, sems)

            @block.sync
            def _(e):
                P.emit("sp", e, sems)
    return nc


_NC_CACHE = {}


def _prep_shared(inp):
    f = lambda k: np.asarray(inp[k], np.float32)
    wb = _host_wblocks(f("w_in")[0], f("w_br_hgrn")[0], f("w_br_fox")[0], f("w_br_mem")[0], f("w_out")[0],
                       f("ffn_w_up")[0], f("ffn_w_down")[0], f("mem_kv_w")[0])
    par = np.zeros((1, NPAR), np.float32)
    gcol = np.zeros((128, 24), np.float32)
    gcol[:, GC_MIX:GC_MIX + 8] = f("norm_mix_g")[0].reshape(8, 128).T
    gcol[:, GC_MEM:GC_MEM + 8] = f("norm_mem_g")[0].reshape(8, 128).T
    gcol[:, GC_FFN:GC_FFN + 8] = f("norm_ffn_g")[0].reshape(8, 128).T
    par[0, PO_HNG:PO_HNG + 128] = f("hgrn_norm_g")[0]
    par[0, PO_FB:PO_FB + 8] = f("fox_f_bias")[0]
    par[0, PO_FQG:PO_FQG + 64] = f("fox_q_norm_g")[0]
    par[0, PO_FKG:PO_FKG + 64] = f("fox_k_norm_g")[0]
    par[0, PO_MQG:PO_MQG + 128] = f("mem_q_norm_g")[0]
    par[0, PO_MKG:PO_MKG + 128] = f("mem_k_norm_g")[0]
    cw = f("ffn_conv_w")[0].reshape(3, NCH, 128)
    cbias = f("ffn_conv_b")[0].reshape(NCH, 128)
    cvp = np.zeros((128, NCH, 4), np.float32)
    cvp[:, :, 0:3] = cw.transpose(2, 1, 0)
    cvp[:, :, 3] = cbias.T
    wff = f("w_in")[0][:, 3584:3592].reshape(8, 128, 8).transpose(1, 0, 2).reshape(128, 64)
    return dict(wblocks=wb, params=par, consts=_host_consts(), convp=np.ascontiguousarray(cvp.reshape(128, NCH * 4)),
                wff=np.ascontiguousarray(wff), gcol=gcol, lblog=np.ascontiguousarray(f("hgrn_lb_logits")))


def kernel(**inputs):
    x = np.asarray(inputs["x"], np.float32)
    mem = np.asarray(inputs["mem"], np.float32)
    B, T, _ = x.shape
    ncores = 8
    nseq = B // ncores
    key = (nseq, T)
    if key not in _NC_CACHE:
        _NC_CACHE[key] = build_nc(nseq, T)
    nc = _NC_CACHE[key]
    shared = _prep_shared(inputs)
    in_maps = []
    for c in range(ncores):
        m = dict(shared)
        m["x"] = np.ascontiguousarray(x[c * nseq:(c + 1) * nseq])
        m["mem"] = np.ascontiguousarray(mem[c * nseq:(c + 1) * nseq])
        in_maps.append(m)
    res = run_bass_kernel_spmd(nc, in_maps, core_ids=list(range(ncores)))
    out = np.concatenate([np.asarray(r["y"]) for r in res.results], axis=0)
    return out.astype(np.float32, copy=False)
```
